# Optimizing a Trainium2 kernel written in Bass

```python
import math
import jax, jax.numpy as jnp
from jax import lax
import numpy as np

D_MODEL = 2048
BATCH = 4
SEQ = 2048
DEPTH = 1

SGU_GROUP_DIM = 128
SGU_WIDTH = D_MODEL // 2
SGU_GROUPS = SGU_WIDTH // SGU_GROUP_DIM
CHUNK = 128
HEAD_DIM = 128
N_HEADS = (D_MODEL // 2) // HEAD_DIM
N_KV_HEADS = 2
GQA_GROUP = N_HEADS // N_KV_HEADS
ATT_WIDTH = N_HEADS * HEAD_DIM
KV_WIDTH = N_KV_HEADS * HEAD_DIM
WINDOW = 128
BLOCK = 128
REL_BUCKETS = 32
REL_MAX_DIST = 128
D_FF = ((8 * D_MODEL // 3 + 255) // 256) * 256
EPS = 1e-6
NEG = -1e30

IN_SPLITS = [SGU_WIDTH, SGU_WIDTH, ATT_WIDTH, KV_WIDTH, KV_WIDTH, D_MODEL, D_MODEL]
IN_COLS = int(sum(IN_SPLITS))
IN_OFFSETS = [int(o) for o in np.cumsum(IN_SPLITS)[:-1]]

kernel_name = "hybrid_sgu_swa_gated_encoder"


def rms_norm(x, g):
    xf = x.astype(jnp.float32)
    y = xf * lax.rsqrt(jnp.mean(xf * xf, axis=-1, keepdims=True) + EPS)
    return (y * g.astype(jnp.float32)).astype(x.dtype)


def t5_bucket(rel):
    nb = REL_BUCKETS // 2
    ret = jnp.where(rel > 0, nb, 0)
    n = jnp.abs(rel)
    max_exact = nb // 2
    nf = jnp.maximum(n, 1).astype(jnp.float32)
    large = max_exact + (jnp.log(nf / max_exact) / math.log(REL_MAX_DIST / max_exact)
                         * (nb - max_exact)).astype(jnp.int32)
    large = jnp.minimum(large, nb - 1)
    return ret + jnp.where(n < max_exact, n, large)


def band_structure(seq):
    nblk = seq // BLOCK
    qi = jnp.arange(BLOCK)[:, None]
    kj = jnp.arange(3 * BLOCK)[None, :]
    rel = kj - BLOCK - qi
    key_pos = jnp.arange(nblk)[:, None, None] * BLOCK + kj[None] - BLOCK
    valid = (jnp.abs(rel)[None] <= WINDOW) & (key_pos >= 0) & (key_pos < seq)
    return rel, valid


def band(t, nblk):
    tp = jnp.pad(t, ((0, 0), (BLOCK, BLOCK), (0, 0), (0, 0)))
    tp = tp.reshape(t.shape[0], nblk + 2, BLOCK, t.shape[2], t.shape[3])
    return jnp.concatenate([tp[:, :-2], tp[:, 1:-1], tp[:, 2:]], axis=2)


def windowed_gqa(q, k, v, rel_bias, sink):
    B, S = q.shape[0], q.shape[1]
    nb = S // BLOCK
    q = q.reshape(B, nb, BLOCK, N_KV_HEADS, GQA_GROUP, HEAD_DIM)
    kb = band(k.reshape(B, S, N_KV_HEADS, HEAD_DIM), nb)
    vb = band(v.reshape(B, S, N_KV_HEADS, HEAD_DIM), nb)
    s = jnp.einsum('bnqkgd,bnjkd->bnkgqj', q, kb).astype(jnp.float32) * (HEAD_DIM ** -0.5)
    rel, valid = band_structure(S)
    bias = rel_bias.astype(jnp.float32)[t5_bucket(rel)]
    bias = bias.transpose(2, 0, 1).reshape(N_KV_HEADS, GQA_GROUP, BLOCK, 3 * BLOCK)
    s = jnp.where(valid[None, :, None, None], s + bias, NEG)
    sink_logit = jnp.broadcast_to(
        sink.astype(jnp.float32).reshape(N_KV_HEADS, GQA_GROUP)[None, None, :, :, None, None],
        s.shape[:-1] + (1,))
    p = jax.nn.softmax(jnp.concatenate([s, sink_logit], axis=-1), axis=-1)[..., :-1]
    o = jnp.einsum('bnkgqj,bnjkd->bnqkgd', p.astype(vb.dtype), vb)
    return o.reshape(B, S, ATT_WIDTH)


def chunked_sgu(u, v, v_gain, w_s, b_s):
    B, S = u.shape[0], u.shape[1]
    nc = S // CHUNK
    v = rms_norm(v, v_gain).reshape(B, nc, CHUNK, SGU_GROUPS, SGU_GROUP_DIM)
    mixed = jnp.einsum('gpq,bcqge->bcpge', w_s, v) + b_s.T[:, :, None]
    return u * mixed.reshape(B, S, SGU_WIDTH)


def setup_inputs(seed: int = 0) -> dict:
    key = jax.random.key(seed)
    ks = jax.random.split(key, 20)
    f32 = jnp.float32

    def nrm(k, shape, scale):
        return jax.random.normal(k, shape, f32) * scale

    return {
        "x": nrm(ks[0], (BATCH, SEQ, D_MODEL), 1.0),
        "w_in": nrm(ks[1], (DEPTH, D_MODEL, IN_COLS), D_MODEL ** -0.5),
        "norm_mix": 1.0 + nrm(ks[2], (DEPTH, D_MODEL), 0.05),
        "sgu_v_gain": 1.0 + nrm(ks[3], (DEPTH, SGU_WIDTH), 0.05),
        "sgu_w_s": nrm(ks[4], (DEPTH, SGU_GROUPS, CHUNK, CHUNK), 0.5 * CHUNK ** -0.5),
        "sgu_b_s": 1.0 + nrm(ks[5], (DEPTH, SGU_GROUPS, CHUNK), 0.1),
        "w_a_out": nrm(ks[6], (DEPTH, SGU_WIDTH, D_MODEL), SGU_WIDTH ** -0.5),
        "attn_sink": nrm(ks[7], (DEPTH, N_HEADS), 0.5),
        "rel_bias": nrm(ks[8], (REL_BUCKETS, N_HEADS), 0.5),
        "w_b_out": nrm(ks[9], (DEPTH, ATT_WIDTH, D_MODEL), ATT_WIDTH ** -0.5),
        "w_o": nrm(ks[10], (DEPTH, D_MODEL, D_MODEL), D_MODEL ** -0.5),
        "norm_ffn": 1.0 + nrm(ks[11], (DEPTH, D_MODEL), 0.05),
        "w_gate": nrm(ks[12], (DEPTH, D_MODEL, D_FF), D_MODEL ** -0.5),
        "w_up": nrm(ks[13], (DEPTH, D_MODEL, D_FF), D_MODEL ** -0.5),
        "w_down": nrm(ks[14], (DEPTH, D_FF, D_MODEL), D_FF ** -0.5),
        "norm_final": 1.0 + nrm(ks[15], (D_MODEL,), 0.05),
    }


def reference(x, w_in, norm_mix, sgu_v_gain, sgu_w_s, sgu_b_s, w_a_out, attn_sink, rel_bias,
              w_b_out, w_o, norm_ffn, w_gate, w_up, w_down, norm_final):
    for l in range(DEPTH):
        h = rms_norm(x, norm_mix[l])
        z = h @ w_in[l]
        zu, zv, q, k, v, ga, gb = jnp.split(z, IN_OFFSETS, axis=-1)
        y_a = chunked_sgu(jax.nn.gelu(zu), jax.nn.gelu(zv), sgu_v_gain[l],
                          sgu_w_s[l], sgu_b_s[l]) @ w_a_out[l]
        y_b = windowed_gqa(q, k, v, rel_bias, attn_sink[l]) @ w_b_out[l]
        m = jax.nn.sigmoid(ga) * y_a + jax.nn.sigmoid(gb) * y_b
        x = x + m @ w_o[l]
        h = rms_norm(x, norm_ffn[l])
        x = x + (jax.nn.silu(h @ w_gate[l]) * (h @ w_up[l])) @ w_down[l]
    return rms_norm(x, norm_final)
```

```python
import contextlib
import math
import numpy as np
import concourse.bass as bass
import concourse.mybir as mybir
from concourse.bass_utils import run_bass_kernel_spmd

F32 = mybir.dt.float32
BF16 = mybir.dt.bfloat16
AF = mybir.ActivationFunctionType
ALU = mybir.AluOpType

NCORES = 8
D = 2048
T = 1024
TH = 1152
NT = 8
IN_COLS = 7680
DFF = 5632
EPS = 1e-6
NEG = -30000.0
SLOT_EL = 4096
NSLOT = 8
FFN_GROUPS = [12, 12, 12, 8]

GELU_FUNC = "tanh"


class Tok:
    __slots__ = ("sem", "key", "val")

    def __init__(self, sem, key, val):
        self.sem, self.key, self.val = sem, key, val


class Buf:
    __slots__ = ("name", "w", "r")

    def __init__(self, name):
        self.name = name
        self.w = None
        self.r = {}


class DSem:
    def __init__(self, h, key):
        self.h, self.key, self.val = h, key, 0


class Eng:
    def __init__(self, name, eng, sem, skip_own=False):
        self.name, self.eng, self.sem = name, eng, sem
        self.key = "E_" + name
        self.n = 0
        self.seen = {}
        self.skip_own = skip_own

    def wait(self, tok):
        if tok is None:
            return
        if self.skip_own and tok.key == self.key:
            return
        if self.seen.get(tok.key, 0) >= tok.val:
            return
        self.eng.wait_ge(tok.sem, tok.val)
        self.seen[tok.key] = tok.val

    def pre(self, reads=(), writes=()):
        for b in reads:
            self.wait(b.w)
        for b in writes:
            self.wait(b.w)
            for t in list(b.r.values()):
                self.wait(t)

    def post(self, inst, reads=(), writes=()):
        self.n += 1
        inst.then_inc(self.sem, 1)
        tok = Tok(self.sem, self.key, self.n)
        for b in reads:
            b.r[self.key] = tok
        for b in writes:
            b.w = tok
            b.r = {}
        return tok

    def op(self, mk, reads=(), writes=()):
        self.pre(reads, writes)
        inst = mk()
        return self.post(inst, reads, writes)

    def dma(self, pairs, dsem, reads=(), writes=()):
        self.pre(reads, writes)
        for (o, i) in pairs:
            self.eng.dma_start(out=o, in_=i).then_inc(dsem.h, 16)
            dsem.val += 16
        tok = Tok(dsem.h, dsem.key, dsem.val)
        for b in reads:
            b.r[dsem.key] = tok
        for b in writes:
            b.w = tok
            b.r = {}
        return tok


def flat(*xs):
    out = []
    for x in xs:
        if isinstance(x, (list, tuple)):
            out.extend(flat(*x))
        else:
            out.append(x)
    return out


def build_nc(debug=None):
    nc = bass.Bass("TRN2", target_bir_lowering=False)
    es = contextlib.ExitStack()

    def dram_in(name, shape):
        return nc.dram_tensor(name, list(shape), F32, kind="ExternalInput").ap()

    xc = dram_in("xc", [TH, D])
    w_in = dram_in("w_in", [D, IN_COLS])
    w_a = dram_in("w_a", [1024, D])
    w_b = dram_in("w_b", [1024, D])
    w_o = dram_in("w_o", [D, D])
    w_gate = dram_in("w_gate", [D, DFF])
    w_up = dram_in("w_up", [D, DFF])
    w_down = dram_in("w_down", [DFF, D])
    wsT_d = dram_in("wsT", [128, 8, 128])
    g_mix_d = dram_in("g_mix_b", [128, D])
    g_ffn_d = dram_in("g_ffn_b", [128, D])
    g_fin_d = dram_in("g_fin_b", [128, D])
    vgain_d = dram_in("vgain_b", [128, 1024])
    bs_d = dram_in("bs_b", [128, 1024])
    sink_d = dram_in("sink_b", [128, 8])
    edge_d = dram_in("edge", [128, 2])
    bias_d = dram_in("biasT", [128, 3 * 8 * 128])
    ident_d = dram_in("ident", [128, 128])
    y = nc.dram_tensor("y", [T, D], F32, kind="ExternalOutput").ap()

    nsem = [0]

    def new_sem(name):
        nsem[0] += 1
        return es.enter_context(nc.semaphore(name))

    def new_dsem(name):
        return DSem(new_sem(name), "D_" + name)

    ARENA_BYTES = 212480
    arena = es.enter_context(nc.sbuf_tensor("arena", [128, ARENA_BYTES // 2], BF16))
    arena_f = arena.bitcast(F32)

    def view(off, shape, dt):
        n = int(np.prod(shape[1:]))
        if dt == BF16:
            assert off % 2 == 0
            ap = arena[:, off // 2: off // 2 + n]
        else:
            assert off % 4 == 0
            ap = arena_f[:, off // 4: off // 4 + n]
        if len(shape) == 3:
            ap = ap.rearrange("p (a b) -> p a b", a=shape[1])
        elif len(shape) == 4:
            ap = ap.rearrange("p (a b c) -> p a b c", a=shape[1], b=shape[2])
        return ap

    class Bump:
        def __init__(self, base):
            self.p = base

        def take(self, shape, dt):
            n = int(np.prod(shape[1:])) * (2 if dt == BF16 else 4)
            off = self.p
            self.p = (off + n + 63) // 64 * 64
            assert self.p <= ARENA_BYTES, (self.p, ARENA_BYTES)
            return view(off, shape, dt)

    def mk_alloc(bump):
        return lambda name, shape, dt: bump.take(list(shape), dt)

    Pb = Bump(0)
    sb = mk_alloc(Pb)

    retired = {}

    def retire(*bufs):
        for b in flat(*bufs):
            for tk in ([b.w] if b.w is not None else []) + list(b.r.values()):
                if tk.key not in retired or retired[tk.key].val < tk.val:
                    retired[tk.key] = tk

    def NB(name):
        b = Buf(name)
        b.r = dict(retired)
        return b

    PE = Eng("pe", nc.tensor, new_sem("s_pe"), skip_own=True)
    ACT = Eng("act", nc.scalar, new_sem("s_act"))
    DVE = Eng("dve", nc.vector, new_sem("s_dve"))
    GQ = Eng("gq", nc.gpsimd, new_sem("s_gq"))
    SP = Eng("sp", nc.sync, new_sem("s_sp"))

    banks = [es.enter_context(nc.psum_tensor(f"bank{i}", [128, 512], F32)) for i in range(8)]
    bank_bufs = [NB(f"bank{i}") for i in range(8)]
    bank_ctr = [0]

    def nb():
        i = bank_ctr[0] % 8
        bank_ctr[0] += 1
        return banks[i], bank_bufs[i]

    slots = [sb(f"slot{i}", [128, SLOT_EL], BF16) for i in range(NSLOT)]
    slot_bufs = [NB(f"slot{i}") for i in range(NSLOT)]
    slot_sems = [new_dsem(f"slot{i}") for i in range(NSLOT)]

    ident = sb("ident", [128, 128], BF16)
    ident_buf = NB("ident")
    ones_bf = sb("ones_bf", [128, 128], BF16)
    ones_buf = NB("ones")
    sink_sb = sb("sink_sb", [128, 8], F32)
    esink = sb("esink", [128, 8], F32)
    esink_buf = NB("esink")
    edge_sb = sb("edge_sb", [128, 2], F32)
    edge_buf = NB("edge")
    GB_OFF = Pb.p
    gb = sb("gb", [128, D], F32)
    junk = sb("junk", [128, D], BF16)
    assert Pb.p == GB_OFF + 12288
    gb_buf = NB("gb")
    gb_sem = new_dsem("gb")
    stat = sb("stat", [128, 128], F32)
    c_sem = new_dsem("consts")
    c2_sem = new_dsem("consts2")

    def v3(slot, k, n, off=0):
        return slot[:, off:off + k * n].rearrange("p (k n) -> p k n", k=k)

    def wrows(w, r0, nk, c0, ncol):
        return w[r0:r0 + nk * 128, c0:c0 + ncol].rearrange("(k p) n -> p k n", p=128)

    entries = []

    def ent(*parts):
        entries.append(list(parts))

    def full_block(w, c0, ncol=256):
        ent((lambda s, ncol=ncol: v3(s, 16, ncol), wrows(w, 0, 16, c0, ncol)))

    for i in range(4):
        full_block(w_in, i * 256)
    full_block(w_in, 3072)
    full_block(w_in, 3328)
    for i in range(4):
        full_block(w_in, 2048 + i * 256)
    for cb in range(2):
        for kh in range(2):
            ent((lambda s: v3(s, 8, 512), wrows(w_in, kh * 1024, 8, 1024 + cb * 512, 512)))
    for jj in range(8):
        full_block(w_in, 3584 + jj * 256)
        full_block(w_in, 5632 + jj * 256)
        ent((lambda s: v3(s, 8, 256, 0), wrows(w_a, 0, 8, jj * 256, 256)),
            (lambda s: v3(s, 8, 256, 2048), wrows(w_b, 0, 8, jj * 256, 256)))
    for n in range(4):
        for kh in range(2):
            ent((lambda s: v3(s, 8, 512), wrows(w_o, kh * 1024, 8, n * 512, 512)))
    f0 = 0
    for gsz in FFN_GROUPS:
        for p in range(gsz // 2):
            full_block(w_gate, (f0 + 2 * p) * 128)
            full_block(w_up, (f0 + 2 * p) * 128)
        for n in range(4):
            ent((lambda s: v3(s, 8, 512), wrows(w_down, f0 * 128, 8, n * 512, 512)))
            if gsz > 8:
                ent((lambda s, k=gsz - 8: v3(s, k, 512), wrows(w_down, (f0 + 8) * 128, gsz - 8, n * 512, 512)))
        f0 += gsz
    assert f0 == 44

    class WStream:
        def __init__(self):
            self.free = list(range(NSLOT))
            self.next_issue = 0
            self.next_get = 0
            self.slot_of = {}

        def pump(self):
            while self.free and self.next_issue < len(entries):
                s = self.free.pop(0)
                e = entries[self.next_issue]
                pairs = [(fn(slots[s]), src) for fn, src in e]
                GQ.dma(pairs, slot_sems[s], reads=[], writes=[slot_bufs[s]])
                self.slot_of[self.next_issue] = s
                self.next_issue += 1

        def get(self):
            i = self.next_get
            assert i in self.slot_of, f"weight entry {i} not issued"
            self.next_get += 1
            return self.slot_of[i]

        def release(self, s):
            self.free.append(s)
            self.pump()

    ws = WStream()

    GQ.dma([(ident[:], ident_d)], c_sem, writes=[ident_buf])
    DVE.op(lambda: nc.vector.memset(ones_bf[:], 1.0), writes=[ones_buf])


    stat_ctr = [0]
    stat0_tok = DVE.op(lambda: nc.vector.memset(stat[:], 0.0))

    _pre = NB("act_preload")
    _pre.w = stat0_tok
    ACT.op(lambda: nc.scalar.activation(out=stat[:, 127:128], in_=stat[:, 127:128], func=AF.Sqrt),
           reads=[_pre], writes=[_pre])

    def stat_col():
        c = stat_ctr[0]
        stat_ctr[0] += 1
        assert c < 128
        b = NB(f"stat{c}")
        b.w = stat0_tok
        return stat[:, c:c + 1], b

    def rs_ts(ss_ap, ss_buf, dim):
        rs_ap, rs_buf = stat_col()
        DVE.op(lambda: nc.vector.tensor_scalar(out=rs_ap, in0=ss_ap, scalar1=1.0 / dim, scalar2=EPS,
                                               op0=ALU.mult, op1=ALU.add),
               reads=[ss_buf], writes=[rs_buf])
        return rs_ap, rs_buf

    def rs_sqrt(rs_ap, rs_buf):
        ACT.op(lambda: nc.scalar.activation(out=rs_ap, in_=rs_ap, func=AF.Sqrt), reads=[rs_buf], writes=[rs_buf])

    def rs_recip(rs_ap, rs_buf):
        DVE.op(lambda: nc.vector.reciprocal(out=rs_ap, in_=rs_ap), reads=[rs_buf], writes=[rs_buf])

    def rstd_from_ss(ss_ap, ss_buf, dim):
        rs_ap, rs_buf = rs_ts(ss_ap, ss_buf, dim)
        rs_sqrt(rs_ap, rs_buf)
        rs_recip(rs_ap, rs_buf)
        return rs_ap, rs_buf

    evac_flip = [0]

    def copy_any(out, in_, reads, writes):
        evac_flip[0] ^= 1
        if evac_flip[0]:
            return ACT.op(lambda: nc.scalar.copy(out=out, in_=in_), reads=reads, writes=writes)
        return DVE.op(lambda: nc.vector.tensor_copy(out=out, in_=in_), reads=reads, writes=writes)

    def b4(bank):
        return bank[:].rearrange("p (a b) -> p a b", a=4)


    junk_buf = NB("junk")
    sqj_holder = [None]

    def sumsq(src_ap, src_bufs, width=D, junk_ap=None):
        ss_ap, ss_buf = stat_col()
        jk = junk[:, 0:width] if junk_ap is None else junk_ap
        jb = junk_buf if junk_ap is None else sqj_holder[0]
        ACT.op(lambda: nc.scalar.activation(out=jk, in_=src_ap, func=AF.Square, accum_out=ss_ap),
               reads=flat(src_bufs, ss_buf), writes=[ss_buf, jb])
        return ss_ap, ss_buf

    def scale_to(dst_ap, dst_bufs, src_ap, src_bufs, rs_ap, rs_buf, gain_ap, gain_buf):
        DVE.op(lambda: nc.vector.scalar_tensor_tensor(out=dst_ap, in0=src_ap, scalar=rs_ap, in1=gain_ap,
                                                      op0=ALU.mult, op1=ALU.mult),
               reads=flat(src_bufs, rs_buf, gain_buf), writes=flat(dst_bufs))

    def tr_pe(hbt, hb_buf):
        res = []
        for q4 in range(4):
            bank, bbuf = nb()
            PE.pre(reads=[hb_buf, ident_buf], writes=[bbuf])
            for i in range(4):
                kc = q4 * 4 + i
                inst = nc.tensor.matmul(bank[:, i * 128:(i + 1) * 128], lhsT=hbt[:, kc * 128:(kc + 1) * 128],
                                        rhs=ident[:], start=True, stop=True)
            PE.post(inst, reads=[hb_buf, ident_buf], writes=[bbuf])
            res.append((bank, bbuf))
        return res

    def tr_copy(res, q4, dstT, dst_bufs_t, tcol, on_dve):
        bank, bbuf = res[q4]
        o_ = dstT[:, q4 * 4:(q4 + 1) * 4, tcol * 128:(tcol + 1) * 128]
        if on_dve:
            DVE.op(lambda: nc.vector.tensor_copy(out=o_, in_=b4(bank)), reads=[bbuf], writes=[dst_bufs_t[q4]])
        else:
            ACT.op(lambda: nc.scalar.copy(out=o_, in_=b4(bank)), reads=[bbuf], writes=[dst_bufs_t[q4]])

    dbg_out = {}

    def dbg_dump(name, ap, shape, bufs, dtype=BF16):
        if debug is None or name not in debug:
            return
        dt = nc.dram_tensor("dbg_" + name, list(shape), dtype, kind="ExternalOutput").ap()
        sem = new_dsem("dbg_" + name)
        tok = SP.dma([(dt, ap)], sem, reads=flat(bufs))
        SP.wait(tok)
        dbg_out[name] = True

    with contextlib.ExitStack() as sA:
        R0 = Pb.p
        Ab = Bump(R0)
        sbA = mk_alloc(Ab)

        hT = sbA("hT", [128, 16, TH], BF16)
        hT_bufs = [[NB(f"hT{t}_{q}") for q in range(4)] for t in range(9)]
        uA = sbA("uA", [128, 8, T], BF16)
        u_bufs = [[NB(f"u{g}_{h}") for h in range(2)] for g in range(8)]
        Bt = sbA("Bt", [128, 8, T], BF16)
        B_bufs = [[NB(f"B{kh}_{n}") for n in range(8)] for kh in range(2)]

        with contextlib.ExitStack() as s0:
            Z0 = Ab.p
            z0 = Bump(Z0)
            NX = 6
            NH = 3
            xt = [z0.take([128, D], F32) for i in range(NX)]
            xt_bufs = [NB(f"xt{i}") for i in range(NX)]
            xt_sems = [new_dsem(f"xt{i}") for i in range(NX)]
            hb = [z0.take([128, D], BF16) for i in range(NH)]
            hb_bufs = [NB(f"hb{i}") for i in range(NH)]

            def p0_load(t):
                return SP.dma([(xt[t % NX][:], xc[t * 128:(t + 1) * 128, :])], xt_sems[t % NX],
                              writes=[xt_bufs[t % NX]])

            ld = [p0_load(0)]
            SP.dma([(gb[:], g_mix_d)], gb_sem, writes=[gb_buf])
            ld += [p0_load(t) for t in range(1, NX)]
            SP.dma([(sink_sb[:], sink_d), (edge_sb[:], edge_d)], c2_sem, writes=[edge_buf, esink_buf])
            GQ.wait(ld[3])
            ws.pump()
            ss = {}
            rs = {}
            trs = {}
            N0 = 9

            def ok(i):
                return 0 <= i < N0

            for t in range(-3, N0 + 1):
                if ok(t):
                    scale_to(hb[t % NH][:], [hb_bufs[t % NH]], xt[t % NX][:], [xt_bufs[t % NX]],
                             rs[t][0], rs[t][1], gb[:], gb_buf)
                    if t + NX < N0:
                        p0_load(t + NX)
                if ok(t + 1):
                    rs_recip(*rs[t + 1])
                if ok(t + 2):
                    rs[t + 2] = rs_ts(*ss[t + 2], D)
                if ok(t - 1):
                    tr_copy(trs[t - 1], 3, hT, hT_bufs[t - 1], t - 1, True)
                if ok(t):
                    trs[t] = tr_pe(hb[t % NH], hb_bufs[t % NH])
                if ok(t - 1):
                    for q4 in range(3):
                        tr_copy(trs[t - 1], q4, hT, hT_bufs[t - 1], t - 1, False)
                if ok(t + 3):
                    ss[t + 3] = sumsq(xt[(t + 3) % NX][:], [xt_bufs[(t + 3) % NX]])
                if ok(t + 2):
                    rs_sqrt(*rs[t + 2])
            dbg_dump("hT", hT[:], [128, 16, TH], hT_bufs)
            retire(xt_bufs, hb_bufs)
            p0_last_sq = Tok(ACT.sem, ACT.key, ACT.n)
        all_barrier = None

        def barrier():
            toks = [Tok(e.sem, e.key, e.n) for e in (PE, ACT, DVE) if e.n > 0]
            for e in (PE, ACT, DVE, GQ, SP):
                for tk in toks:
                    if e.skip_own and tk.key == e.key:
                        continue
                    e.wait(tk)

        hT_all = flat(hT_bufs[:8])
        hT_half = [flat(hT_bufs[0:4]), flat(hT_bufs[4:8])]

        with contextlib.ExitStack() as s1:
            b1 = Bump(Z0)
            sb1 = mk_alloc(b1)

            qT = sb1("qT", [128, 8, T], BF16)
            q_bufs = [[NB(f"q{h}_{hf}") for hf in range(2)] for h in range(8)]
            kT = sb1("kT", [128, 2, TH], BF16)
            k_bufs = [[NB(f"k{kh}_{i}") for i in range(3)] for kh in range(2)]
            vtok = sb1("vtok", [128, 9, 256], BF16)
            v_bufs = [NB(f"v{t}") for t in range(9)]

            biasT = view(GB_OFF, [128, 3, 8, 128], F32)
            bias_buf = gb_buf
            SP.dma([(biasT[:], bias_d.rearrange("p (a h q) -> p a h q", a=3, h=8))], gb_sem,
                   writes=[gb_buf, junk_buf])
            E2 = sb1("E2", [128, 8, 128], BF16)
            e2_buf = NB("E2")
            tmpb = Bump(b1.p)
            esink_full = tmpb.take([128, 8, 128], F32)
            es_tmp = tmpb.take([128, 8, 128], BF16)
            ACT.op(lambda: nc.scalar.activation(out=esink[:], in_=sink_sb[:], func=AF.Exp),
                   reads=[esink_buf], writes=[esink_buf])
            DVE.op(lambda: nc.vector.memset(esink_full[:], 0.0), reads=[esink_buf], writes=[e2_buf])
            for h in range(8):
                DVE.op(lambda h=h: nc.vector.tensor_scalar(out=esink_full[:, h, :], in0=esink_full[:, h, :],
                                                           scalar1=esink[:, h:h + 1], scalar2=None, op0=ALU.add),
                       reads=[esink_buf, e2_buf], writes=[e2_buf])
            DVE.op(lambda: nc.vector.tensor_scalar(out=E2[0:32], in0=esink_full[0:32], scalar1=1.0 / 32, scalar2=None,
                                                   op0=ALU.mult), reads=[e2_buf], writes=[e2_buf])
            DVE.op(lambda: nc.vector.tensor_scalar(out=es_tmp[32:64], in0=esink_full[32:64], scalar1=1.0 / 32,
                                                   scalar2=None, op0=ALU.mult), reads=[e2_buf], writes=[e2_buf])
            DVE.op(lambda: nc.vector.scalar_tensor_tensor(out=E2[32:64], in0=esink_full[32:64], scalar=1.0 / 32,
                                                          in1=es_tmp[32:64], op0=ALU.mult, op1=ALU.subtract),
                   reads=[e2_buf], writes=[e2_buf])
            retire(e2_buf)


            def fm_chunk(si, c, evac, extra_halo=False):
                w = v3(slots[si], 16, 256)
                bk = [nb(), nb()]
                if extra_halo:
                    bk.append(nb())
                reads = flat(slot_bufs[si], hT_all, hT_bufs[8] if extra_halo else [])
                PE.pre(reads=reads, writes=[b for _, b in bk])
                for kc in range(16):
                    for hf in range(len(bk)):
                        if hf < 2:
                            rhs = hT[:, kc, hf * 512:(hf + 1) * 512]
                            o = bk[hf][0][:, :]
                        else:
                            rhs = hT[:, kc, 1024:1152]
                            o = bk[hf][0][:, 0:128]
                        inst = nc.tensor.matmul(o, lhsT=w[:, kc, c * 128:(c + 1) * 128], rhs=rhs,
                                                start=(kc == 0), stop=(kc == 15))
                PE.post(inst, reads=reads, writes=[b for _, b in bk])
                for hf in range(len(bk)):
                    evac(hf, bk[hf][0], bk[hf][1])

            gelu_f = AF.Gelu_apprx_tanh if GELU_FUNC == "tanh" else AF.Gelu
            early_ga = {}

            def fm_half_outer(items, src, src_half_bufs, nk):
                for hf in range(2):
                    for (w, c, sbuf_, evac) in items:
                        bank, bbuf = nb()
                        reads = flat(sbuf_, src_half_bufs[hf])
                        PE.pre(reads=reads, writes=[bbuf])
                        for kc in range(nk):
                            inst = nc.tensor.matmul(bank[:, :], lhsT=w[:, kc, c * 128:(c + 1) * 128],
                                                    rhs=src[:, kc, hf * 512:(hf + 1) * 512],
                                                    start=(kc == 0), stop=(kc == nk - 1))
                        PE.post(inst, reads=reads, writes=[bbuf])
                        evac(hf, bank, bbuf)

            def zu_ev(g):
                def ev(hf, bank, bbuf):
                    ACT.op(lambda: nc.scalar.activation(out=uA[:, g, hf * 512:(hf + 1) * 512], in_=bank[:, :],
                                                        func=gelu_f),
                           reads=[bbuf], writes=[u_bufs[g][hf]])
                return ev

            s01 = [ws.get(), ws.get()]
            fm_half_outer([(v3(slots[s01[b_]], 16, 256), c, slot_bufs[s01[b_]], zu_ev(b_ * 2 + c))
                           for b_ in range(2) for c in range(2)], hT, hT_half, 16)
            for si in s01:
                ws.release(si)
            for blk in range(2, 4):
                si = ws.get()
                for c in range(2):
                    fm_chunk(si, c, zu_ev(blk * 2 + c))
                ws.release(si)
            dbg_dump("u", uA[:], [128, 8, T], u_bufs)

            si = ws.get()
            for c in range(2):
                def ev(hf, bank, bbuf, c=c):
                    if hf < 2:
                        copy_any(kT[:, c, hf * 512:(hf + 1) * 512], bank[:, :], [bbuf], [k_bufs[c][hf]])
                    else:
                        copy_any(kT[:, c, 1024:1152], bank[:, 0:128], [bbuf], [k_bufs[c][2]])
                fm_chunk(si, c, ev, extra_halo=True)
            ws.release(si)

            si = ws.get()
            w = v3(slots[si], 16, 256)
            for t in range(9):
                bank, bbuf = nb()
                reads = flat(slot_bufs[si], hT_bufs[t])
                PE.pre(reads=reads, writes=[bbuf])
                for kc in range(16):
                    inst = nc.tensor.matmul(bank[:, 0:256], lhsT=hT[:, kc, t * 128:(t + 1) * 128], rhs=w[:, kc, :],
                                            start=(kc == 0), stop=(kc == 15))
                PE.post(inst, reads=reads, writes=[bbuf])
                copy_any(vtok[:, t, :], bank[:, 0:256], [bbuf], [v_bufs[t]])
            ws.release(si)

            qscale = 128 ** -0.5
            for blk in range(4):
                si = ws.get()
                for c in range(2):
                    h = blk * 2 + c

                    def ev(hf, bank, bbuf, h=h):
                        ACT.op(lambda: nc.scalar.activation(out=qT[:, h, hf * 512:(hf + 1) * 512], in_=bank[:, :],
                                                            func=AF.Copy, scale=qscale),
                               reads=[bbuf], writes=[q_bufs[h][hf]])
                    fm_chunk(si, c, ev)
                ws.release(si)
            dbg_dump("qT", qT[:], [128, 8, T], q_bufs)
            dbg_dump("kT", kT[:], [128, 2, TH], k_bufs)
            dbg_dump("vtok", vtok[:], [128, 9, 256], v_bufs)

            with contextlib.ExitStack() as s1c:
                sbc = mk_alloc(Bump(b1.p))

                NSC = 6
                sc = [sbc(f"sc{i}", [128, 512], F32) for i in range(NSC)]
                sc_bufs = [NB(f"sc{i}") for i in range(NSC)]
                pT = [sbc(f"pT{i}", [128, 512], BF16) for i in range(9)]
                pT_bufs = [NB(f"pT{i}") for i in range(9)]
                lnd = [sbc(f"lnd{i}", [128, 512], F32) for i in range(2)]
                lnd_bufs = [NB(f"lnd{i}") for i in range(2)]
                rec = [sbc(f"rec{i}", [128, 512], F32) for i in range(2)]
                rec_bufs = [NB(f"rec{i}") for i in range(2)]
                its = [(n, kh) for n in range(NT) for kh in range(2)]

                def kbs_of(n):
                    return [(n - 1) if n > 0 else 8, n, (n + 1) if n < 7 else 8]

                def ring(base, n):
                    ctr = [0]

                    def take():
                        i = base + ctr[0] % n
                        ctr[0] += 1
                        return banks[i], bank_bufs[i]
                    return take

                nb_score = ring(0, 3)
                nb_den = ring(3, 2)
                nb_o = ring(5, 3)
                ob = {}
                db = {}

                def att_t1(it):
                    n, kh = its[it]
                    kbs = kbs_of(n)
                    qrd = [q_bufs[kh * 4 + g][n // 4] for g in range(4)]
                    pset = (it % 3) * 3
                    for j in range(3):
                        kb = kbs[j]
                        bank, bbuf = nb_score()
                        krd = k_bufs[kh][kb // 4]
                        PE.pre(reads=flat(krd, qrd), writes=[bbuf])
                        inst = nc.tensor.matmul(b4(bank), lhsT=kT[:, kh, kb * 128:(kb + 1) * 128],
                                                rhs=qT[:, kh * 4:(kh + 1) * 4, n * 128:(n + 1) * 128],
                                                start=True, stop=True)
                        PE.post(inst, reads=flat(krd, qrd), writes=[bbuf])
                        si_ = (it * 3 + j) % NSC
                        DVE.op(lambda: nc.vector.tensor_tensor(out=sc[si_][:].rearrange("p (a b) -> p a b", a=4),
                                                               in0=b4(bank), in1=biasT[:, j, kh * 4:(kh + 1) * 4, :],
                                                               op=ALU.add),
                               reads=[bbuf, bias_buf, junk_buf], writes=[sc_bufs[si_]])
                        if n == 0 and j == 0:
                            eb = edge_sb[:, 0:1]
                        elif n == 7 and j == 2:
                            eb = edge_sb[:, 1:2]
                        else:
                            eb = None
                        pt = pT[pset + j]
                        if eb is None:
                            ACT.op(lambda: nc.scalar.activation(out=pt[:], in_=sc[si_][:], func=AF.Exp),
                                   reads=[sc_bufs[si_]], writes=[pT_bufs[pset + j]])
                        else:
                            ACT.op(lambda: nc.scalar.activation(out=pt[:], in_=sc[si_][:], func=AF.Exp, bias=eb),
                                   reads=[sc_bufs[si_], edge_buf], writes=[pT_bufs[pset + j]])

                def att_t2a(it):
                    n, kh = its[it]
                    kbs = kbs_of(n)
                    pset = (it % 3) * 3
                    obank, obuf = nb_o()
                    dbank, dbuf = nb_den()
                    ob[it] = (obank, obuf)
                    db[it] = (dbank, dbuf)
                    prd = [pT_bufs[pset + j] for j in range(3)]
                    vrd = [v_bufs[kb] for kb in kbs]
                    PE.pre(reads=flat(prd, vrd, ones_buf, e2_buf), writes=[obuf, dbuf])
                    for j in range(3):
                        nc.tensor.matmul(obank[:, :], lhsT=vtok[:, kbs[j], kh * 128:(kh + 1) * 128],
                                         rhs=pT[pset + j][:], start=(j == 0), stop=(j == 2))
                    for j in range(3):
                        nc.tensor.matmul(dbank[:, :], lhsT=ones_bf[:], rhs=pT[pset + j][:],
                                         start=(j == 0), stop=False)
                    inst = nc.tensor.matmul(b4(dbank), lhsT=ones_bf[0:64, :], rhs=E2[0:64, kh * 4:(kh + 1) * 4, :],
                                            start=False, stop=True)
                    PE.post(inst, reads=flat(prd, vrd, ones_buf, e2_buf), writes=[obuf, dbuf])

                def att_t2b(it):
                    dbank, dbuf = db[it]
                    k_ = it % 2
                    ACT.op(lambda: nc.scalar.activation(out=lnd[k_][:], in_=dbank[:, :], func=AF.Ln),
                           reads=[dbuf], writes=[lnd_bufs[k_]])
                    ACT.op(lambda: nc.scalar.activation(out=rec[k_][:], in_=lnd[k_][:], func=AF.Exp, scale=-1.0),
                           reads=[lnd_bufs[k_]], writes=[rec_bufs[k_]])

                def att_t2c(it):
                    n, kh = its[it]
                    obank, obuf = ob[it]
                    k_ = it % 2
                    DVE.op(lambda: nc.vector.tensor_tensor(out=Bt[:, kh * 4:(kh + 1) * 4, n * 128:(n + 1) * 128],
                                                           in0=b4(obank),
                                                           in1=rec[k_][:].rearrange("p (a b) -> p a b", a=4),
                                                           op=ALU.mult),
                           reads=[obuf, rec_bufs[k_]], writes=[B_bufs[kh][n]])

                NI = 16
                for step in range(-2, NI + 2):
                    if 0 <= step + 2 < NI:
                        att_t1(step + 2)
                    if 0 <= step + 1 < NI:
                        att_t2a(step + 1)
                    if 0 <= step < NI:
                        att_t2b(step)
                    if 0 <= step - 1 < NI:
                        att_t2c(step - 1)
                dbg_dump("B", Bt[:], [128, 8, T], B_bufs)
                retire(e2_buf, sc_bufs, pT_bufs, lnd_bufs, rec_bufs, q_bufs, k_bufs, v_bufs)
            with contextlib.ExitStack() as s1b:
                sbb = mk_alloc(Bump(b1.p))

                wsT = sbb("wsT", [128, 8, 128], BF16)
                wsT_buf = NB("wsT")
                vgain = sbb("vgain", [128, 1024], F32)
                bsb = sbb("bsb", [128, 8, 128], F32)
                cb_buf = NB("sgu_consts")
                sg_sem = new_dsem("sguc")
                sg2_sem = new_dsem("sguc2")
                GQ.dma([(wsT[:], wsT_d)], sg_sem, writes=[wsT_buf])
                SP.dma([(vgain[:], vgain_d), (bsb[:], bs_d.rearrange("p (g q) -> p g q", g=8))], sg2_sem,
                       writes=[cb_buf])
                zvt = [sbb(f"zvt{i}", [128, 1024], F32) for i in range(4)]
                zvt_bufs = [NB(f"zvt{i}") for i in range(4)]
                vn = [sbb(f"vn{i}", [128, 1024], BF16) for i in range(2)]
                vn_bufs = [NB(f"vn{i}") for i in range(2)]
                sqj = sbb("sqj", [128, 1024], BF16)
                sqj_holder[0] = NB("sqj")

                zs = [ws.get() for _ in range(4)]
                sg_ss = {}
                sg_rs = {}
                last_sq = [None]

                def sgu_s1(t):
                    zb = t % 4
                    bks = [nb(), nb()]
                    reads = flat([slot_bufs[z] for z in zs], hT_bufs[t])
                    PE.pre(reads=reads, writes=[b for _, b in bks])
                    for cbi in range(2):
                        for kc in range(16):
                            wv = v3(slots[zs[cbi * 2 + kc // 8]], 8, 512)
                            inst = nc.tensor.matmul(bks[cbi][0][:, :], lhsT=hT[:, kc, t * 128:(t + 1) * 128],
                                                    rhs=wv[:, kc % 8, :], start=(kc == 0), stop=(kc == 15))
                    PE.post(inst, reads=reads, writes=[b for _, b in bks])
                    for cbi in range(2):
                        ACT.op(lambda cbi=cbi: nc.scalar.activation(out=zvt[zb][:, cbi * 512:(cbi + 1) * 512],
                                                                    in_=bks[cbi][0][:, :], func=gelu_f),
                               reads=[bks[cbi][1]], writes=[zvt_bufs[zb]])
                    sg_ss[t] = sumsq(zvt[zb][:], [zvt_bufs[zb]], width=1024, junk_ap=sqj[:])

                def sgu_s2(t):
                    sg_rs[t] = rstd_from_ss(*sg_ss[t], 1024)

                def sgu_s3a(t):
                    zb = t % 4
                    s = t % 2
                    rs_ap, rs_buf = sg_rs[t]
                    scale_to(vn[s][:], [vn_bufs[s]], zvt[zb][:], [zvt_bufs[zb]], rs_ap, rs_buf, vgain[:], cb_buf)

                def sgu_s3(t):
                    s = t % 2
                    for b2 in range(2):
                        bank, bbuf = nb()
                        PE.pre(reads=[vn_bufs[s], wsT_buf], writes=[bbuf])
                        for i in range(4):
                            g = b2 * 4 + i
                            inst = nc.tensor.matmul(bank[:, i * 128:(i + 1) * 128],
                                                    lhsT=vn[s][:, g * 128:(g + 1) * 128], rhs=wsT[:, g, :],
                                                    start=True, stop=True)
                        PE.post(inst, reads=[vn_bufs[s], wsT_buf], writes=[bbuf])
                        ub = [u_bufs[b2 * 4 + i][t // 4] for i in range(4)]
                        DVE.op(lambda: nc.vector.tensor_tensor(out=b4(bank), in0=b4(bank),
                                                               in1=bsb[:, b2 * 4:(b2 + 1) * 4, :], op=ALU.add),
                               reads=[bbuf, cb_buf], writes=[bbuf])
                        usl = uA[:, b2 * 4:(b2 + 1) * 4, t * 128:(t + 1) * 128]
                        DVE.op(lambda: nc.vector.tensor_tensor(out=usl, in0=b4(bank), in1=usl, op=ALU.mult),
                               reads=flat(bbuf, ub), writes=ub)

                sgu_s1(0)
                sgu_s1(1)
                for p in range(4):
                    sgu_s2(2 * p)
                    sgu_s2(2 * p + 1)
                    if p + 1 < 4:
                        sgu_s1(2 * p + 2)
                        sgu_s1(2 * p + 3)
                    else:
                        s_ga0 = ws.get()
                        w_ = v3(slots[s_ga0], 16, 256)
                        bk_ = [nb(), nb()]
                        rd_ = flat(slot_bufs[s_ga0], hT_half)
                        PE.pre(reads=rd_, writes=[b for _, b in bk_])
                        for kc in range(16):
                            for hf in range(2):
                                inst = nc.tensor.matmul(bk_[hf][0][:, :], lhsT=w_[:, kc, 0:128],
                                                        rhs=hT[:, kc, hf * 512:(hf + 1) * 512],
                                                        start=(kc == 0), stop=(kc == 15))
                        PE.post(inst, reads=rd_, writes=[b for _, b in bk_])
                        early_ga["slot"] = s_ga0
                        early_ga["bk"] = bk_
                    sgu_s3a(2 * p)
                    sgu_s3a(2 * p + 1)
                    sgu_s3(2 * p)
                    sgu_s3(2 * p + 1)
                for z in zs:
                    ws.release(z)
                dbg_dump("A", uA[:], [128, 8, T], u_bufs)
                retire(wsT_buf, cb_buf, zvt_bufs, vn_bufs)


        b2 = Bump(Z0)
        mT = b2.take([128, 16, T], BF16)
        MT_END = b2.p
        mT_bufs = [[NB(f"mT{t}_{q}") for q in range(4)] for t in range(8)]

        def mT_bufs_for(j, hf):
            return [mT_bufs[t][j // 4] for t in range(hf * 4, hf * 4 + 4)]

        with contextlib.ExitStack() as s2:
            sb2 = mk_alloc(b2)

            sig = [sb2(f"sig{i}", [128, 512], F32) for i in range(4)]
            sig_bufs = [NB(f"sig{i}") for i in range(4)]
            t1 = [sb2(f"t1_{i}", [128, 512], F32) for i in range(4)]
            t1_bufs = [NB(f"t1_{i}") for i in range(4)]
            A_all = flat(u_bufs)
            B_all = flat(B_bufs)
            B_half = [flat([B_bufs[kh][n] for kh in range(2) for n in range(hf * 4, hf * 4 + 4)]) for hf in range(2)]
            A_half = [[u_bufs[g][hf] for g in range(8)] for hf in range(2)]
            sctr = 0
            for jj in range(8):
                s_ga = early_ga["slot"] if jj == 0 else ws.get()
                s_gb = ws.get()
                s_ab = ws.get()
                for c in range(2):
                    j = jj * 2 + c
                    res = {}
                    for name, si, nk, off, src, src_bufs in (
                        ("ga", s_ga, 16, 0, hT, hT_half),
                        ("gb", s_gb, 16, 0, hT, hT_half),
                        ("ya", s_ab, 8, 0, uA, A_half),
                        ("yb", s_ab, 8, 2048, Bt, B_half),
                    ):
                        w = v3(slots[si], nk, 256, off)
                        if j == 0 and name == "ga":
                            bk = early_ga["bk"]
                        else:
                            bk = [nb(), nb()]
                            reads = flat(slot_bufs[si], src_bufs)
                            PE.pre(reads=reads, writes=[b for _, b in bk])
                            for kc in range(nk):
                                for hf in range(2):
                                    inst = nc.tensor.matmul(bk[hf][0][:, :], lhsT=w[:, kc, c * 128:(c + 1) * 128],
                                                            rhs=src[:, kc, hf * 512:(hf + 1) * 512],
                                                            start=(kc == 0), stop=(kc == nk - 1))
                            PE.post(inst, reads=reads, writes=[b for _, b in bk])
                        res[name] = bk
                        if name in ("ga", "gb"):
                            for hf in range(2):
                                k_ = (0 if name == "ga" else 2) + hf
                                ACT.op(lambda: nc.scalar.activation(out=sig[k_][:], in_=bk[hf][0][:, :], func=AF.Sigmoid),
                                       reads=[bk[hf][1]], writes=[sig_bufs[k_]])
                        elif name == "ya":
                            for hf in range(2):
                                DVE.op(lambda: nc.vector.tensor_tensor(out=t1[hf][:], in0=sig[hf][:], in1=bk[hf][0][:, :],
                                                                       op=ALU.mult),
                                       reads=[sig_bufs[hf], bk[hf][1]], writes=[t1_bufs[hf]])
                        else:
                            for hf in range(2):
                                DVE.op(lambda: nc.vector.tensor_tensor(out=t1[2 + hf][:], in0=sig[2 + hf][:],
                                                                       in1=bk[hf][0][:, :], op=ALU.mult),
                                       reads=[sig_bufs[2 + hf], bk[hf][1]], writes=[t1_bufs[2 + hf]])
                                DVE.op(lambda: nc.vector.tensor_tensor(out=mT[:, j, hf * 512:(hf + 1) * 512],
                                                                       in0=t1[hf][:], in1=t1[2 + hf][:], op=ALU.add),
                                       reads=[t1_bufs[hf], t1_bufs[2 + hf]], writes=mT_bufs_for(j, hf))
                ws.release(s_ga)
                ws.release(s_gb)
                ws.release(s_ab)
            dbg_dump("mT", mT[:], [128, 16, T], mT_bufs)
            retire(hT_bufs, u_bufs, B_bufs, sig_bufs, t1_bufs)

    bd = Bump(R0)
    x1 = bd.take([128, NT, D], F32)
    sgt01 = [bd.take([128, 512], F32) for i in range(2)]
    assert bd.p <= Z0, (bd.p, Z0)
    x1_bufs = [[NB(f"x1_{t}_{n}") for n in range(4)] for t in range(NT)]
    x1_sem = new_dsem("x1ld")
    for n in range(4):
        for t in range(NT):
            SP.dma([(x1[:, t, n * 512:(n + 1) * 512], xc[t * 128:(t + 1) * 128, n * 512:(n + 1) * 512])],
                   new_dsem(f"x1ld{n}_{t}"), writes=[x1_bufs[t][n]])

    bd2 = Bump(MT_END)
    hb2 = [bd2.take([128, D], BF16) for i in range(2)]
    hb2_bufs = [NB(f"hb2_{i}") for i in range(2)]

    d2_ss = {}
    d2_rs = {}

    d2_tr = {}

    def d2_stage_a(t):
        d2_rs[t] = rs_ts(*d2_ss[t], D)
        rs_sqrt(*d2_rs[t])

    def d2_stage_b(t):
        rs_recip(*d2_rs[t])
        scale_to(hb2[t % 2][:], [hb2_bufs[t % 2]], x1[:, t, :], x1_bufs[t], d2_rs[t][0], d2_rs[t][1], gb[:], gb_buf)

    def d2_stage_c(t):
        d2_tr[t] = tr_pe(hb2[t % 2], hb2_bufs[t % 2])
        for q4 in range(4):
            tr_copy(d2_tr[t], q4, mT, mT_bufs[t], t, q4 == 3)

    for n in range(4):
        sl = [ws.get(), ws.get()]
        for t in range(NT):
            bank, bbuf = nb()
            reads = flat([slot_bufs[s_] for s_ in sl], mT_bufs[t])
            PE.pre(reads=reads, writes=[bbuf])
            for kc in range(16):
                wv = v3(slots[sl[kc // 8]], 8, 512)
                inst = nc.tensor.matmul(bank[:, :], lhsT=mT[:, kc, t * 128:(t + 1) * 128], rhs=wv[:, kc % 8, :],
                                        start=(kc == 0), stop=(kc == 15))
            PE.post(inst, reads=reads, writes=[bbuf])
            xs = x1[:, t, n * 512:(n + 1) * 512]
            DVE.op(lambda: nc.vector.tensor_tensor(out=xs, in0=xs, in1=bank[:, :], op=ALU.add),
                   reads=[bbuf, x1_bufs[t][n]], writes=[x1_bufs[t][n]])
            if n == 3:
                if t == 0:
                    SP.dma([(gb[:], g_ffn_d)], gb_sem, writes=[gb_buf])
                if t >= 1:
                    d2_stage_a(t - 1)
                d2_ss[t] = sumsq(x1[:, t, :], x1_bufs[t])
                if t >= 1:
                    d2_stage_b(t - 1)
                if t >= 2:
                    d2_stage_c(t - 2)
        for s_ in sl:
            ws.release(s_)
    d2_stage_a(NT - 1)
    d2_stage_c(NT - 2)
    early_ffn = {}
    s_g0 = ws.get()
    s_u0 = ws.get()
    for name_, si_ in (("g", s_g0), ("u", s_u0)):
        w_ = v3(slots[si_], 16, 256)
        bank_, bbuf_ = nb()
        rd_ = flat(slot_bufs[si_], mT_bufs[0:4])
        PE.pre(reads=rd_, writes=[bbuf_])
        for kc in range(16):
            inst = nc.tensor.matmul(bank_[:, :], lhsT=w_[:, kc, 0:128], rhs=mT[:, kc, 0:512],
                                    start=(kc == 0), stop=(kc == 15))
        PE.post(inst, reads=rd_, writes=[bbuf_])
        early_ffn[name_] = (bank_, bbuf_)
    d2_stage_b(NT - 1)
    d2_stage_c(NT - 1)
    dbg_dump("x1", x1[:], [128, NT, D], x1_bufs, F32)
    dbg_dump("h2T", mT[:], [128, 16, T], mT_bufs)
    h2T = mT
    h2_half = [flat(mT_bufs[0:4]), flat(mT_bufs[4:8])]
    retire(hb2_bufs)

    bf_ = Bump(MT_END)
    actT = bf_.take([128, 12, T], BF16)
    act_bufs = [[NB(f"act{f}_{hf}") for hf in range(2)] for f in range(12)]
    sgt = sgt01 + [bf_.take([128, 512], F32) for i in range(2)]
    sgt_bufs = [NB(f"sgt{i}") for i in range(4)]
    sgc = 0
    fin_ss = {}
    out_sem = new_dsem("out")
    out_toks = []

    def fin_stage_b(t):
        rs_ap, rs_buf = rstd_from_ss(*fin_ss[t], D)
        if t == NT - 1:
            for hh in range(2):
                c0, c1 = hh * 1024, (hh + 1) * 1024
                bufs = x1_bufs[t][2 * hh:2 * hh + 2]
                scale_to(x1[:, t, c0:c1], bufs, x1[:, t, c0:c1], bufs, rs_ap, rs_buf, gb[:, c0:c1], gb_buf)
                out_toks.append(SP.dma([(y[t * 128:(t + 1) * 128, c0:c1], x1[:, t, c0:c1])], out_sem, reads=bufs))
            return
        scale_to(x1[:, t, :], x1_bufs[t], x1[:, t, :], x1_bufs[t], rs_ap, rs_buf, gb[:], gb_buf)
        out_toks.append(SP.dma([(y[t * 128:(t + 1) * 128, :], x1[:, t, :])], out_sem, reads=x1_bufs[t]))

    for gi, gsz in enumerate(FFN_GROUPS):
        last_group = (gi == len(FFN_GROUPS) - 1)
        for p in range(gsz // 2):
            if gi == 0 and p == 0:
                s_g, s_u = s_g0, s_u0
            else:
                s_g = ws.get()
                s_u = ws.get()
            if gi == 0 and p == 0:
                for hf in range(2):
                    for c in range(2):
                        fi = c
                        k_ = None
                        for name, si in (("g", s_g), ("u", s_u)):
                            w = v3(slots[si], 16, 256)
                            if hf == 0 and c == 0:
                                bank, bbuf = early_ffn[name]
                            else:
                                bank, bbuf = nb()
                                reads = flat(slot_bufs[si], h2_half[hf])
                                PE.pre(reads=reads, writes=[bbuf])
                                for kc in range(16):
                                    inst = nc.tensor.matmul(bank[:, :], lhsT=w[:, kc, c * 128:(c + 1) * 128],
                                                            rhs=h2T[:, kc, hf * 512:(hf + 1) * 512],
                                                            start=(kc == 0), stop=(kc == 15))
                                PE.post(inst, reads=reads, writes=[bbuf])
                            if name == "g":
                                k_ = sgc % 4
                                sgc += 1
                                ACT.op(lambda: nc.scalar.activation(out=sgt[k_][:], in_=bank[:, :], func=AF.Silu),
                                       reads=[bbuf], writes=[sgt_bufs[k_]])
                            else:
                                DVE.op(lambda: nc.vector.tensor_tensor(out=actT[:, fi, hf * 512:(hf + 1) * 512],
                                                                       in0=sgt[k_][:], in1=bank[:, :], op=ALU.mult),
                                       reads=[sgt_bufs[k_], bbuf], writes=[act_bufs[fi][hf]])
                ws.release(s_g)
                ws.release(s_u)
                continue
            for c in range(2):
                fi = 2 * p + c
                for name, si in (("g", s_g), ("u", s_u)):
                    w = v3(slots[si], 16, 256)
                    bk = [nb(), nb()]
                    reads = flat(slot_bufs[si], h2_half)
                    PE.pre(reads=reads, writes=[b for _, b in bk])
                    for kc in range(16):
                        for hf in range(2):
                            inst = nc.tensor.matmul(bk[hf][0][:, :], lhsT=w[:, kc, c * 128:(c + 1) * 128],
                                                    rhs=h2T[:, kc, hf * 512:(hf + 1) * 512],
                                                    start=(kc == 0), stop=(kc == 15))
                    PE.post(inst, reads=reads, writes=[b for _, b in bk])
                    if name == "g":
                        ks = []
                        for hf in range(2):
                            k_ = sgc % 4
                            sgc += 1
                            ks.append(k_)
                            ACT.op(lambda: nc.scalar.activation(out=sgt[k_][:], in_=bk[hf][0][:, :], func=AF.Silu),
                                   reads=[bk[hf][1]], writes=[sgt_bufs[k_]])
                    else:
                        for hf in range(2):
                            k_ = ks[hf]
                            DVE.op(lambda: nc.vector.tensor_tensor(out=actT[:, fi, hf * 512:(hf + 1) * 512],
                                                                   in0=sgt[k_][:], in1=bk[hf][0][:, :], op=ALU.mult),
                                   reads=[sgt_bufs[k_], bk[hf][1]], writes=[act_bufs[fi][hf]])
            ws.release(s_g)
            ws.release(s_u)

        def down_group(n, t, sl):
            bank, bbuf = nb()
            reads = flat([slot_bufs[s_] for s_ in sl], [act_bufs[f][t // 4] for f in range(gsz)])
            PE.pre(reads=reads, writes=[bbuf])
            for f in range(gsz):
                if f < 8:
                    wv = v3(slots[sl[0]], 8, 512)[:, f, :]
                else:
                    wv = v3(slots[sl[1]], gsz - 8, 512)[:, f - 8, :]
                inst = nc.tensor.matmul(bank[:, :], lhsT=actT[:, f, t * 128:(t + 1) * 128], rhs=wv,
                                        start=(f == 0), stop=(f == gsz - 1))
            PE.post(inst, reads=reads, writes=[bbuf])
            xs = x1[:, t, n * 512:(n + 1) * 512]
            DVE.op(lambda: nc.vector.tensor_tensor(out=xs, in0=xs, in1=bank[:, :], op=ALU.add),
                   reads=[bbuf, x1_bufs[t][n]], writes=[x1_bufs[t][n]])

        if not last_group:
            for n in range(4):
                sl = [ws.get()]
                if gsz > 8:
                    sl.append(ws.get())
                for t in range(NT):
                    down_group(n, t, sl)
                for s_ in sl:
                    ws.release(s_)
        else:
            assert gsz <= 8
            SP.dma([(gb[:], g_fin_d)], gb_sem, writes=[gb_buf])
            sls = [[ws.get()] for n in range(4)]
            for t in range(NT):
                for n in range(4):
                    down_group(n, t, sls[n])
                if t >= 1:
                    fin_stage_b(t - 1)
                fin_ss[t] = sumsq(x1[:, t, :], x1_bufs[t])
            for n in range(4):
                ws.release(sls[n][0])
            fin_stage_b(NT - 1)

    SP.wait(out_toks[-1])
    es.close()
    return nc


def _t5_bucket_np(rel):
    nb = 16
    ret = np.where(rel > 0, nb, 0)
    n = np.abs(rel)
    me = 8
    nf = np.maximum(n, 1).astype(np.float32)
    large = me + (np.log(nf / np.float32(me)) / np.float32(math.log(128 / me)) * np.float32(nb - me)).astype(np.int32)
    large = np.minimum(large, nb - 1)
    return ret + np.where(n < me, n, large)


def _bias_table(rel_bias):
    j = np.arange(128)[:, None]
    i = np.arange(128)[None, :]
    out = np.empty((128, 3, 8, 128), np.float32)
    for ty, off in enumerate((-128, 0, 128)):
        rel = j + off - i
        valid = np.abs(rel) <= 128
        bk = _t5_bucket_np(rel)
        g = rel_bias[bk]
        g = np.where(valid[:, :, None], g, np.float32(NEG)).astype(np.float32)
        out[:, ty] = g.transpose(0, 2, 1)
    return out.reshape(128, 3 * 8 * 128)


def _rep(v, n=128):
    return np.ascontiguousarray(np.broadcast_to(np.asarray(v, np.float32).reshape(1, -1), (n, v.size)))


_NC_CACHE = {}


def make_in_maps(x, w_in, norm_mix, sgu_v_gain, sgu_w_s, sgu_b_s, w_a_out, attn_sink, rel_bias,
                 w_b_out, w_o, norm_ffn, w_gate, w_up, w_down, norm_final):
    f = lambda a: np.ascontiguousarray(np.asarray(a, dtype=np.float32))
    x = f(x)
    shared = {
        "w_in": f(w_in)[0], "w_a": f(w_a_out)[0], "w_b": f(w_b_out)[0], "w_o": f(w_o)[0],
        "w_gate": f(w_gate)[0], "w_up": f(w_up)[0], "w_down": f(w_down)[0],
        "wsT": np.ascontiguousarray(f(sgu_w_s)[0].transpose(2, 0, 1)),
        "g_mix_b": _rep(f(norm_mix)[0]), "g_ffn_b": _rep(f(norm_ffn)[0]), "g_fin_b": _rep(f(norm_final)),
        "vgain_b": _rep(f(sgu_v_gain)[0]), "bs_b": _rep(f(sgu_b_s)[0].reshape(-1)),
        "sink_b": _rep(f(attn_sink)[0]),
        "biasT": _bias_table(f(rel_bias)),
        "ident": np.eye(128, dtype=np.float32),
    }
    in_maps = []
    for c in range(NCORES):
        b, half = c // 2, c % 2
        own = x[b, half * 1024:(half + 1) * 1024]
        halo = x[b, 1024:1152] if half == 0 else x[b, 896:1024]
        edge = np.zeros((128, 2), np.float32)
        if half == 0:
            edge[:, 0] = NEG
        else:
            edge[:, 1] = NEG
        m = dict(shared)
        m["xc"] = np.ascontiguousarray(np.concatenate([own, halo], axis=0))
        m["edge"] = edge
        in_maps.append(m)
    return in_maps


def kernel(**inputs):
    in_maps = make_in_maps(**inputs)
    if "nc" not in _NC_CACHE:
        _NC_CACHE["nc"] = build_nc()
    nc = _NC_CACHE["nc"]
    res = run_bass_kernel_spmd(nc, in_maps, core_ids=list(range(NCORES)))
    out = np.empty((4, 2048, D), np.float32)
    for c in range(NCORES):
        b, half = c // 2, c % 2
        out[b, half * 1024:(half + 1) * 1024] = res.results[c]["y"]
    return out
```

```python
import contextlib
import math
import numpy as np
import concourse.bass as bass
import concourse.mybir as mybir
from concourse.bass_utils import run_bass_kernel_spmd

F32 = mybir.dt.float32
BF16 = mybir.dt.bfloat16
AF = mybir.ActivationFunctionType
ALU = mybir.AluOpType

NCORES = 8
D = 2048
T = 1024
TH = 1152
NT = 8
IN_COLS = 7680
DFF = 5632
EPS = 1e-6
NEG = -30000.0
SLOT_EL = 4096
NSLOT = 8
FFN_GROUPS = [12, 12, 12, 8]

GELU_FUNC = "tanh"


class Tok:
    __slots__ = ("sem", "key", "val")

    def __init__(self, sem, key, val):
        self.sem, self.key, self.val = sem, key, val


class Buf:
    __slots__ = ("name", "w", "r")

    def __init__(self, name):
        self.name = name
        self.w = None
        self.r = {}


class DSem:
    def __init__(self, h, key):
        self.h, self.key, self.val = h, key, 0


class Eng:
    def __init__(self, name, eng, sem, skip_own=False):
        self.name, self.eng, self.sem = name, eng, sem
        self.key = "E_" + name
        self.n = 0
        self.seen = {}
        self.skip_own = skip_own

    def wait(self, tok):
        if tok is None:
            return
        if self.skip_own and tok.key == self.key:
            return
        if self.seen.get(tok.key, 0) >= tok.val:
            return
        self.eng.wait_ge(tok.sem, tok.val)
        self.seen[tok.key] = tok.val

    def pre(self, reads=(), writes=()):
        for b in reads:
            self.wait(b.w)
        for b in writes:
            self.wait(b.w)
            for t in list(b.r.values()):
                self.wait(t)

    def post(self, inst, reads=(), writes=()):
        self.n += 1
        inst.then_inc(self.sem, 1)
        tok = Tok(self.sem, self.key, self.n)
        for b in reads:
            b.r[self.key] = tok
        for b in writes:
            b.w = tok
            b.r = {}
        return tok

    def op(self, mk, reads=(), writes=()):
        self.pre(reads, writes)
        inst = mk()
        return self.post(inst, reads, writes)

    def dma(self, pairs, dsem, reads=(), writes=()):
        self.pre(reads, writes)
        for (o, i) in pairs:
            self.eng.dma_start(out=o, in_=i).then_inc(dsem.h, 16)
            dsem.val += 16
        tok = Tok(dsem.h, dsem.key, dsem.val)
        for b in reads:
            b.r[dsem.key] = tok
        for b in writes:
            b.w = tok
            b.r = {}
        return tok


def flat(*xs):
    out = []
    for x in xs:
        if isinstance(x, (list, tuple)):
            out.extend(flat(*x))
        else:
            out.append(x)
    return out


def build_nc(debug=None):
    nc = bass.Bass("TRN2", target_bir_lowering=False)
    es = contextlib.ExitStack()

    def dram_in(name, shape):
        return nc.dram_tensor(name, list(shape), F32, kind="ExternalInput").ap()

    xc = dram_in("xc", [TH, D])
    w_in = dram_in("w_in", [D, IN_COLS])
    w_a = dram_in("w_a", [1024, D])
    w_b = dram_in("w_b", [1024, D])
    w_o = dram_in("w_o", [D, D])
    w_gate = dram_in("w_gate", [D, DFF])
    w_up = dram_in("w_up", [D, DFF])
    w_down = dram_in("w_down", [DFF, D])
    wsT_d = dram_in("wsT", [128, 8, 128])
    g_mix_d = dram_in("g_mix_b", [128, D])
    g_ffn_d = dram_in("g_ffn_b", [128, D])
    g_fin_d = dram_in("g_fin_b", [128, D])
    vgain_d = dram_in("vgain_b", [128, 1024])
    bs_d = dram_in("bs_b", [128, 1024])
    sink_d = dram_in("sink_b", [128, 8])
    edge_d = dram_in("edge", [128, 2])
    bias_d = dram_in("biasT", [128, 3 * 8 * 128])
    ident_d = dram_in("ident", [128, 128])
    y = nc.dram_tensor("y", [T, D], F32, kind="ExternalOutput").ap()

    nsem = [0]

    def new_sem(name):
        nsem[0] += 1
        return es.enter_context(nc.semaphore(name))

    def new_dsem(name):
        return DSem(new_sem(name), "D_" + name)

    ARENA_BYTES = 212480
    arena = es.enter_context(nc.sbuf_tensor("arena", [128, ARENA_BYTES // 2], BF16))
    arena_f = arena.bitcast(F32)

    def view(off, shape, dt):
        n = int(np.prod(shape[1:]))
        if dt == BF16:
            assert off % 2 == 0
            ap = arena[:, off // 2: off // 2 + n]
        else:
            assert off % 4 == 0
            ap = arena_f[:, off // 4: off // 4 + n]
        if len(shape) == 3:
            ap = ap.rearrange("p (a b) -> p a b", a=shape[1])
        elif len(shape) == 4:
            ap = ap.rearrange("p (a b c) -> p a b c", a=shape[1], b=shape[2])
        return ap

    class Bump:
        def __init__(self, base):
            self.p = base

        def take(self, shape, dt):
            n = int(np.prod(shape[1:])) * (2 if dt == BF16 else 4)
            off = self.p
            self.p = (off + n + 63) // 64 * 64
            assert self.p <= ARENA_BYTES, (self.p, ARENA_BYTES)
            return view(off, shape, dt)

    def mk_alloc(bump):
        return lambda name, shape, dt: bump.take(list(shape), dt)

    Pb = Bump(0)
    sb = mk_alloc(Pb)

    retired = {}

    def retire(*bufs):
        for b in flat(*bufs):
            for tk in ([b.w] if b.w is not None else []) + list(b.r.values()):
                if tk.key not in retired or retired[tk.key].val < tk.val:
                    retired[tk.key] = tk

    def NB(name):
        b = Buf(name)
        b.r = dict(retired)
        return b

    PE = Eng("pe", nc.tensor, new_sem("s_pe"), skip_own=True)
    ACT = Eng("act", nc.scalar, new_sem("s_act"))
    DVE = Eng("dve", nc.vector, new_sem("s_dve"))
    GQ = Eng("gq", nc.gpsimd, new_sem("s_gq"))
    SP = Eng("sp", nc.sync, new_sem("s_sp"))

    banks = [es.enter_context(nc.psum_tensor(f"bank{i}", [128, 512], F32)) for i in range(8)]
    bank_bufs = [NB(f"bank{i}") for i in range(8)]
    bank_ctr = [0]

    def nb():
        i = bank_ctr[0] % 8
        bank_ctr[0] += 1
        return banks[i], bank_bufs[i]

    slots = [sb(f"slot{i}", [128, SLOT_EL], BF16) for i in range(NSLOT)]
    slot_bufs = [NB(f"slot{i}") for i in range(NSLOT)]
    slot_sems = [new_dsem(f"slot{i}") for i in range(NSLOT)]

    ident = sb("ident", [128, 128], BF16)
    ident_buf = NB("ident")
    ones_bf = sb("ones_bf", [128, 128], BF16)
    ones_buf = NB("ones")
    sink_sb = sb("sink_sb", [128, 8], F32)
    esink = sb("esink", [128, 8], F32)
    esink_buf = NB("esink")
    edge_sb = sb("edge_sb", [128, 2], F32)
    edge_buf = NB("edge")
    GB_OFF = Pb.p
    gb = sb("gb", [128, D], F32)
    junk = sb("junk", [128, D], BF16)
    assert Pb.p == GB_OFF + 12288
    gb_buf = NB("gb")
    gb_sem = new_dsem("gb")
    stat = sb("stat", [128, 128], F32)
    c_sem = new_dsem("consts")
    c2_sem = new_dsem("consts2")

    def v3(slot, k, n, off=0):
        return slot[:, off:off + k * n].rearrange("p (k n) -> p k n", k=k)

    def wrows(w, r0, nk, c0, ncol):
        return w[r0:r0 + nk * 128, c0:c0 + ncol].rearrange("(k p) n -> p k n", p=128)

    entries = []

    def ent(*parts):
        entries.append(list(parts))

    def full_block(w, c0, ncol=256):
        ent((lambda s, ncol=ncol: v3(s, 16, ncol), wrows(w, 0, 16, c0, ncol)))

    for i in range(4):
        full_block(w_in, i * 256)
    full_block(w_in, 3072)
    full_block(w_in, 3328)
    for i in range(4):
        full_block(w_in, 2048 + i * 256)
    for cb in range(2):
        for kh in range(2):
            ent((lambda s: v3(s, 8, 512), wrows(w_in, kh * 1024, 8, 1024 + cb * 512, 512)))
    for jj in range(8):
        full_block(w_in, 3584 + jj * 256)
        full_block(w_in, 5632 + jj * 256)
        ent((lambda s: v3(s, 8, 256, 0), wrows(w_a, 0, 8, jj * 256, 256)),
            (lambda s: v3(s, 8, 256, 2048), wrows(w_b, 0, 8, jj * 256, 256)))
    for n in range(4):
        for kh in range(2):
            ent((lambda s: v3(s, 8, 512), wrows(w_o, kh * 1024, 8, n * 512, 512)))
    f0 = 0
    for gsz in FFN_GROUPS:
        for p in range(gsz // 2):
            full_block(w_gate, (f0 + 2 * p) * 128)
            full_block(w_up, (f0 + 2 * p) * 128)
        for n in range(4):
            ent((lambda s: v3(s, 8, 512), wrows(w_down, f0 * 128, 8, n * 512, 512)))
            if gsz > 8:
                ent((lambda s, k=gsz - 8: v3(s, k, 512), wrows(w_down, (f0 + 8) * 128, gsz - 8, n * 512, 512)))
        f0 += gsz
    assert f0 == 44

    class WStream:
        def __init__(self):
            self.free = list(range(NSLOT))
            self.next_issue = 0
            self.next_get = 0
            self.slot_of = {}

        def pump(self):
            while self.free and self.next_issue < len(entries):
                s = self.free.pop(0)
                e = entries[self.next_issue]
                pairs = [(fn(slots[s]), src) for fn, src in e]
                GQ.dma(pairs, slot_sems[s], reads=[], writes=[slot_bufs[s]])
                self.slot_of[self.next_issue] = s
                self.next_issue += 1

        def get(self):
            i = self.next_get
            assert i in self.slot_of, f"weight entry {i} not issued"
            self.next_get += 1
            return self.slot_of[i]

        def release(self, s):
            self.free.append(s)
            self.pump()

    ws = WStream()

    GQ.dma([(ident[:], ident_d)], c_sem, writes=[ident_buf])
    DVE.op(lambda: nc.vector.memset(ones_bf[:], 1.0), writes=[ones_buf])


    stat_ctr = [0]
    stat0_tok = DVE.op(lambda: nc.vector.memset(stat[:], 0.0))

    def stat_col():
        c = stat_ctr[0]
        stat_ctr[0] += 1
        assert c < 128
        b = NB(f"stat{c}")
        b.w = stat0_tok
        return stat[:, c:c + 1], b

    def rs_ts(ss_ap, ss_buf, dim):
        rs_ap, rs_buf = stat_col()
        DVE.op(lambda: nc.vector.tensor_scalar(out=rs_ap, in0=ss_ap, scalar1=1.0 / dim, scalar2=EPS,
                                               op0=ALU.mult, op1=ALU.add),
               reads=[ss_buf], writes=[rs_buf])
        return rs_ap, rs_buf

    def rs_sqrt(rs_ap, rs_buf):
        ACT.op(lambda: nc.scalar.activation(out=rs_ap, in_=rs_ap, func=AF.Sqrt), reads=[rs_buf], writes=[rs_buf])

    def rs_recip(rs_ap, rs_buf):
        DVE.op(lambda: nc.vector.reciprocal(out=rs_ap, in_=rs_ap), reads=[rs_buf], writes=[rs_buf])

    def rstd_from_ss(ss_ap, ss_buf, dim):
        rs_ap, rs_buf = rs_ts(ss_ap, ss_buf, dim)
        rs_sqrt(rs_ap, rs_buf)
        rs_recip(rs_ap, rs_buf)
        return rs_ap, rs_buf

    evac_flip = [0]

    def copy_any(out, in_, reads, writes):
        evac_flip[0] ^= 1
        if evac_flip[0]:
            return ACT.op(lambda: nc.scalar.copy(out=out, in_=in_), reads=reads, writes=writes)
        return DVE.op(lambda: nc.vector.tensor_copy(out=out, in_=in_), reads=reads, writes=writes)

    def b4(bank):
        return bank[:].rearrange("p (a b) -> p a b", a=4)


    junk_buf = NB("junk")
    sqj_holder = [None]

    def sumsq(src_ap, src_bufs, width=D, junk_ap=None):
        ss_ap, ss_buf = stat_col()
        jk = junk[:, 0:width] if junk_ap is None else junk_ap
        jb = junk_buf if junk_ap is None else sqj_holder[0]
        ACT.op(lambda: nc.scalar.activation(out=jk, in_=src_ap, func=AF.Square, accum_out=ss_ap),
               reads=flat(src_bufs, ss_buf), writes=[ss_buf, jb])
        return ss_ap, ss_buf

    def scale_to(dst_ap, dst_bufs, src_ap, src_bufs, rs_ap, rs_buf, gain_ap, gain_buf):
        DVE.op(lambda: nc.vector.scalar_tensor_tensor(out=dst_ap, in0=src_ap, scalar=rs_ap, in1=gain_ap,
                                                      op0=ALU.mult, op1=ALU.mult),
               reads=flat(src_bufs, rs_buf, gain_buf), writes=flat(dst_bufs))

    def tr_pe(hbt, hb_buf):
        res = []
        for q4 in range(4):
            bank, bbuf = nb()
            PE.pre(reads=[hb_buf, ident_buf], writes=[bbuf])
            for i in range(4):
                kc = q4 * 4 + i
                inst = nc.tensor.matmul(bank[:, i * 128:(i + 1) * 128], lhsT=hbt[:, kc * 128:(kc + 1) * 128],
                                        rhs=ident[:], start=True, stop=True)
            PE.post(inst, reads=[hb_buf, ident_buf], writes=[bbuf])
            res.append((bank, bbuf))
        return res

    def tr_copy(res, q4, dstT, dst_bufs_t, tcol, on_dve):
        bank, bbuf = res[q4]
        o_ = dstT[:, q4 * 4:(q4 + 1) * 4, tcol * 128:(tcol + 1) * 128]
        if on_dve:
            DVE.op(lambda: nc.vector.tensor_copy(out=o_, in_=b4(bank)), reads=[bbuf], writes=[dst_bufs_t[q4]])
        else:
            ACT.op(lambda: nc.scalar.copy(out=o_, in_=b4(bank)), reads=[bbuf], writes=[dst_bufs_t[q4]])

    dbg_out = {}

    def dbg_dump(name, ap, shape, bufs, dtype=BF16):
        if debug is None or name not in debug:
            return
        dt = nc.dram_tensor("dbg_" + name, list(shape), dtype, kind="ExternalOutput").ap()
        sem = new_dsem("dbg_" + name)
        tok = SP.dma([(dt, ap)], sem, reads=flat(bufs))
        SP.wait(tok)
        dbg_out[name] = True

    with contextlib.ExitStack() as sA:
        R0 = Pb.p
        Ab = Bump(R0)
        sbA = mk_alloc(Ab)

        hT = sbA("hT", [128, 16, TH], BF16)
        hT_bufs = [[NB(f"hT{t}_{q}") for q in range(4)] for t in range(9)]
        uA = sbA("uA", [128, 8, T], BF16)
        u_bufs = [[NB(f"u{g}_{h}") for h in range(2)] for g in range(8)]
        Bt = sbA("Bt", [128, 8, T], BF16)
        B_bufs = [[NB(f"B{kh}_{n}") for n in range(8)] for kh in range(2)]

        with contextlib.ExitStack() as s0:
            Z0 = Ab.p
            z0 = Bump(Z0)
            NX = 6
            NH = 3
            xt = [z0.take([128, D], F32) for i in range(NX)]
            xt_bufs = [NB(f"xt{i}") for i in range(NX)]
            xt_sems = [new_dsem(f"xt{i}") for i in range(NX)]
            hb = [z0.take([128, D], BF16) for i in range(NH)]
            hb_bufs = [NB(f"hb{i}") for i in range(NH)]

            def p0_load(t):
                return SP.dma([(xt[t % NX][:], xc[t * 128:(t + 1) * 128, :])], xt_sems[t % NX],
                              writes=[xt_bufs[t % NX]])

            ld = [p0_load(0)]
            SP.dma([(gb[:], g_mix_d)], gb_sem, writes=[gb_buf])
            ld += [p0_load(t) for t in range(1, NX)]
            SP.dma([(sink_sb[:], sink_d), (edge_sb[:], edge_d)], c2_sem, writes=[edge_buf, esink_buf])
            GQ.wait(ld[1])
            ws.pump()
            ss = {}
            rs = {}
            trs = {}
            N0 = 9

            def ok(i):
                return 0 <= i < N0

            for t in range(-3, N0 + 1):
                if ok(t):
                    scale_to(hb[t % NH][:], [hb_bufs[t % NH]], xt[t % NX][:], [xt_bufs[t % NX]],
                             rs[t][0], rs[t][1], gb[:], gb_buf)
                    if t + NX < N0:
                        p0_load(t + NX)
                if ok(t + 1):
                    rs_recip(*rs[t + 1])
                if ok(t + 2):
                    rs[t + 2] = rs_ts(*ss[t + 2], D)
                if ok(t - 1):
                    tr_copy(trs[t - 1], 3, hT, hT_bufs[t - 1], t - 1, True)
                if ok(t):
                    trs[t] = tr_pe(hb[t % NH], hb_bufs[t % NH])
                if ok(t - 1):
                    for q4 in range(3):
                        tr_copy(trs[t - 1], q4, hT, hT_bufs[t - 1], t - 1, False)
                if ok(t + 3):
                    ss[t + 3] = sumsq(xt[(t + 3) % NX][:], [xt_bufs[(t + 3) % NX]])
                if ok(t + 2):
                    rs_sqrt(*rs[t + 2])
            dbg_dump("hT", hT[:], [128, 16, TH], hT_bufs)
            retire(xt_bufs, hb_bufs)
            p0_last_sq = Tok(ACT.sem, ACT.key, ACT.n)
        all_barrier = None

        def barrier():
            toks = [Tok(e.sem, e.key, e.n) for e in (PE, ACT, DVE) if e.n > 0]
            for e in (PE, ACT, DVE, GQ, SP):
                for tk in toks:
                    if e.skip_own and tk.key == e.key:
                        continue
                    e.wait(tk)

        hT_all = flat(hT_bufs[:8])
        hT_half = [flat(hT_bufs[0:4]), flat(hT_bufs[4:8])]

        with contextlib.ExitStack() as s1:
            b1 = Bump(Z0)
            sb1 = mk_alloc(b1)

            qT = sb1("qT", [128, 8, T], BF16)
            q_bufs = [[NB(f"q{h}_{hf}") for hf in range(2)] for h in range(8)]
            kT = sb1("kT", [128, 2, TH], BF16)
            k_bufs = [[NB(f"k{kh}_{i}") for i in range(3)] for kh in range(2)]
            vtok = sb1("vtok", [128, 9, 256], BF16)
            v_bufs = [NB(f"v{t}") for t in range(9)]

            biasT = view(GB_OFF, [128, 3, 8, 128], F32)
            bias_buf = gb_buf
            SP.dma([(biasT[:], bias_d.rearrange("p (a h q) -> p a h q", a=3, h=8))], gb_sem,
                   writes=[gb_buf, junk_buf])
            E2 = sb1("E2", [128, 8, 128], BF16)
            e2_buf = NB("E2")
            tmpb = Bump(b1.p)
            esink_full = tmpb.take([128, 8, 128], F32)
            es_tmp = tmpb.take([128, 8, 128], BF16)
            ACT.op(lambda: nc.scalar.activation(out=esink[:], in_=sink_sb[:], func=AF.Exp),
                   reads=[esink_buf], writes=[esink_buf])
            DVE.op(lambda: nc.vector.memset(esink_full[:], 0.0), reads=[esink_buf], writes=[e2_buf])
            for h in range(8):
                DVE.op(lambda h=h: nc.vector.tensor_scalar(out=esink_full[:, h, :], in0=esink_full[:, h, :],
                                                           scalar1=esink[:, h:h + 1], scalar2=None, op0=ALU.add),
                       reads=[esink_buf, e2_buf], writes=[e2_buf])
            DVE.op(lambda: nc.vector.tensor_scalar(out=E2[0:32], in0=esink_full[0:32], scalar1=1.0 / 32, scalar2=None,
                                                   op0=ALU.mult), reads=[e2_buf], writes=[e2_buf])
            DVE.op(lambda: nc.vector.tensor_scalar(out=es_tmp[32:64], in0=esink_full[32:64], scalar1=1.0 / 32,
                                                   scalar2=None, op0=ALU.mult), reads=[e2_buf], writes=[e2_buf])
            DVE.op(lambda: nc.vector.scalar_tensor_tensor(out=E2[32:64], in0=esink_full[32:64], scalar=1.0 / 32,
                                                          in1=es_tmp[32:64], op0=ALU.mult, op1=ALU.subtract),
                   reads=[e2_buf], writes=[e2_buf])
            retire(e2_buf)


            def fm_chunk(si, c, evac, extra_halo=False):
                w = v3(slots[si], 16, 256)
                bk = [nb(), nb()]
                if extra_halo:
                    bk.append(nb())
                reads = flat(slot_bufs[si], hT_all, hT_bufs[8] if extra_halo else [])
                PE.pre(reads=reads, writes=[b for _, b in bk])
                for kc in range(16):
                    for hf in range(len(bk)):
                        if hf < 2:
                            rhs = hT[:, kc, hf * 512:(hf + 1) * 512]
                            o = bk[hf][0][:, :]
                        else:
                            rhs = hT[:, kc, 1024:1152]
                            o = bk[hf][0][:, 0:128]
                        inst = nc.tensor.matmul(o, lhsT=w[:, kc, c * 128:(c + 1) * 128], rhs=rhs,
                                                start=(kc == 0), stop=(kc == 15))
                PE.post(inst, reads=reads, writes=[b for _, b in bk])
                for hf in range(len(bk)):
                    evac(hf, bk[hf][0], bk[hf][1])

            gelu_f = AF.Gelu_apprx_tanh if GELU_FUNC == "tanh" else AF.Gelu
            early_ga = {}

            def fm_half_outer(items, src, src_half_bufs, nk):
                for hf in range(2):
                    for (w, c, sbuf_, evac) in items:
                        bank, bbuf = nb()
                        reads = flat(sbuf_, src_half_bufs[hf])
                        PE.pre(reads=reads, writes=[bbuf])
                        for kc in range(nk):
                            inst = nc.tensor.matmul(bank[:, :], lhsT=w[:, kc, c * 128:(c + 1) * 128],
                                                    rhs=src[:, kc, hf * 512:(hf + 1) * 512],
                                                    start=(kc == 0), stop=(kc == nk - 1))
                        PE.post(inst, reads=reads, writes=[bbuf])
                        evac(hf, bank, bbuf)

            def zu_ev(g):
                def ev(hf, bank, bbuf):
                    ACT.op(lambda: nc.scalar.activation(out=uA[:, g, hf * 512:(hf + 1) * 512], in_=bank[:, :],
                                                        func=gelu_f),
                           reads=[bbuf], writes=[u_bufs[g][hf]])
                return ev

            s01 = [ws.get(), ws.get()]
            fm_half_outer([(v3(slots[s01[b_]], 16, 256), c, slot_bufs[s01[b_]], zu_ev(b_ * 2 + c))
                           for b_ in range(2) for c in range(2)], hT, hT_half, 16)
            for si in s01:
                ws.release(si)
            for blk in range(2, 4):
                si = ws.get()
                for c in range(2):
                    fm_chunk(si, c, zu_ev(blk * 2 + c))
                ws.release(si)
            dbg_dump("u", uA[:], [128, 8, T], u_bufs)

            si = ws.get()
            for c in range(2):
                def ev(hf, bank, bbuf, c=c):
                    if hf < 2:
                        copy_any(kT[:, c, hf * 512:(hf + 1) * 512], bank[:, :], [bbuf], [k_bufs[c][hf]])
                    else:
                        copy_any(kT[:, c, 1024:1152], bank[:, 0:128], [bbuf], [k_bufs[c][2]])
                fm_chunk(si, c, ev, extra_halo=True)
            ws.release(si)

            si = ws.get()
            w = v3(slots[si], 16, 256)
            for t in range(9):
                bank, bbuf = nb()
                reads = flat(slot_bufs[si], hT_bufs[t])
                PE.pre(reads=reads, writes=[bbuf])
                for kc in range(16):
                    inst = nc.tensor.matmul(bank[:, 0:256], lhsT=hT[:, kc, t * 128:(t + 1) * 128], rhs=w[:, kc, :],
                                            start=(kc == 0), stop=(kc == 15))
                PE.post(inst, reads=reads, writes=[bbuf])
                copy_any(vtok[:, t, :], bank[:, 0:256], [bbuf], [v_bufs[t]])
            ws.release(si)

            qscale = 128 ** -0.5
            for blk in range(4):
                si = ws.get()
                for c in range(2):
                    h = blk * 2 + c

                    def ev(hf, bank, bbuf, h=h):
                        ACT.op(lambda: nc.scalar.activation(out=qT[:, h, hf * 512:(hf + 1) * 512], in_=bank[:, :],
                                                            func=AF.Copy, scale=qscale),
                               reads=[bbuf], writes=[q_bufs[h][hf]])
                    fm_chunk(si, c, ev)
                ws.release(si)
            dbg_dump("qT", qT[:], [128, 8, T], q_bufs)
            dbg_dump("kT", kT[:], [128, 2, TH], k_bufs)
            dbg_dump("vtok", vtok[:], [128, 9, 256], v_bufs)

            with contextlib.ExitStack() as s1c:
                sbc = mk_alloc(Bump(b1.p))

                NSC = 6
                sc = [sbc(f"sc{i}", [128, 512], F32) for i in range(NSC)]
                sc_bufs = [NB(f"sc{i}") for i in range(NSC)]
                pT = [sbc(f"pT{i}", [128, 512], BF16) for i in range(9)]
                pT_bufs = [NB(f"pT{i}") for i in range(9)]
                lnd = [sbc(f"lnd{i}", [128, 512], F32) for i in range(2)]
                lnd_bufs = [NB(f"lnd{i}") for i in range(2)]
                rec = [sbc(f"rec{i}", [128, 512], F32) for i in range(2)]
                rec_bufs = [NB(f"rec{i}") for i in range(2)]
                its = [(n, kh) for n in range(NT) for kh in range(2)]

                def kbs_of(n):
                    return [(n - 1) if n > 0 else 8, n, (n + 1) if n < 7 else 8]

                def ring(base, n):
                    ctr = [0]

                    def take():
                        i = base + ctr[0] % n
                        ctr[0] += 1
                        return banks[i], bank_bufs[i]
                    return take

                nb_score = ring(0, 3)
                nb_den = ring(3, 2)
                nb_o = ring(5, 3)
                ob = {}
                db = {}

                def att_t1(it):
                    n, kh = its[it]
                    kbs = kbs_of(n)
                    qrd = [q_bufs[kh * 4 + g][n // 4] for g in range(4)]
                    pset = (it % 3) * 3
                    for j in range(3):
                        kb = kbs[j]
                        bank, bbuf = nb_score()
                        krd = k_bufs[kh][kb // 4]
                        PE.pre(reads=flat(krd, qrd), writes=[bbuf])
                        inst = nc.tensor.matmul(b4(bank), lhsT=kT[:, kh, kb * 128:(kb + 1) * 128],
                                                rhs=qT[:, kh * 4:(kh + 1) * 4, n * 128:(n + 1) * 128],
                                                start=True, stop=True)
                        PE.post(inst, reads=flat(krd, qrd), writes=[bbuf])
                        si_ = (it * 3 + j) % NSC
                        DVE.op(lambda: nc.vector.tensor_tensor(out=sc[si_][:].rearrange("p (a b) -> p a b", a=4),
                                                               in0=b4(bank), in1=biasT[:, j, kh * 4:(kh + 1) * 4, :],
                                                               op=ALU.add),
                               reads=[bbuf, bias_buf, junk_buf], writes=[sc_bufs[si_]])
                        if n == 0 and j == 0:
                            eb = edge_sb[:, 0:1]
                        elif n == 7 and j == 2:
                            eb = edge_sb[:, 1:2]
                        else:
                            eb = None
                        pt = pT[pset + j]
                        if eb is None:
                            ACT.op(lambda: nc.scalar.activation(out=pt[:], in_=sc[si_][:], func=AF.Exp),
                                   reads=[sc_bufs[si_]], writes=[pT_bufs[pset + j]])
                        else:
                            ACT.op(lambda: nc.scalar.activation(out=pt[:], in_=sc[si_][:], func=AF.Exp, bias=eb),
                                   reads=[sc_bufs[si_], edge_buf], writes=[pT_bufs[pset + j]])

                def att_t2a(it):
                    n, kh = its[it]
                    kbs = kbs_of(n)
                    pset = (it % 3) * 3
                    obank, obuf = nb_o()
                    dbank, dbuf = nb_den()
                    ob[it] = (obank, obuf)
                    db[it] = (dbank, dbuf)
                    prd = [pT_bufs[pset + j] for j in range(3)]
                    vrd = [v_bufs[kb] for kb in kbs]
                    PE.pre(reads=flat(prd, vrd, ones_buf, e2_buf), writes=[obuf, dbuf])
                    for j in range(3):
                        nc.tensor.matmul(obank[:, :], lhsT=vtok[:, kbs[j], kh * 128:(kh + 1) * 128],
                                         rhs=pT[pset + j][:], start=(j == 0), stop=(j == 2))
                    for j in range(3):
                        nc.tensor.matmul(dbank[:, :], lhsT=ones_bf[:], rhs=pT[pset + j][:],
                                         start=(j == 0), stop=False)
                    inst = nc.tensor.matmul(b4(dbank), lhsT=ones_bf[0:64, :], rhs=E2[0:64, kh * 4:(kh + 1) * 4, :],
                                            start=False, stop=True)
                    PE.post(inst, reads=flat(prd, vrd, ones_buf, e2_buf), writes=[obuf, dbuf])

                def att_t2b(it):
                    dbank, dbuf = db[it]
                    k_ = it % 2
                    ACT.op(lambda: nc.scalar.activation(out=lnd[k_][:], in_=dbank[:, :], func=AF.Ln),
                           reads=[dbuf], writes=[lnd_bufs[k_]])
                    ACT.op(lambda: nc.scalar.activation(out=rec[k_][:], in_=lnd[k_][:], func=AF.Exp, scale=-1.0),
                           reads=[lnd_bufs[k_]], writes=[rec_bufs[k_]])

                def att_t2c(it):
                    n, kh = its[it]
                    obank, obuf = ob[it]
                    k_ = it % 2
                    DVE.op(lambda: nc.vector.tensor_tensor(out=Bt[:, kh * 4:(kh + 1) * 4, n * 128:(n + 1) * 128],
                                                           in0=b4(obank),
                                                           in1=rec[k_][:].rearrange("p (a b) -> p a b", a=4),
                                                           op=ALU.mult),
                           reads=[obuf, rec_bufs[k_]], writes=[B_bufs[kh][n]])

                NI = 16
                for step in range(-2, NI + 2):
                    if 0 <= step + 2 < NI:
                        att_t1(step + 2)
                    if 0 <= step + 1 < NI:
                        att_t2a(step + 1)
                    if 0 <= step < NI:
                        att_t2b(step)
                    if 0 <= step - 1 < NI:
                        att_t2c(step - 1)
                dbg_dump("B", Bt[:], [128, 8, T], B_bufs)
                retire(e2_buf, sc_bufs, pT_bufs, lnd_bufs, rec_bufs, q_bufs, k_bufs, v_bufs)
            with contextlib.ExitStack() as s1b:
                sbb = mk_alloc(Bump(b1.p))

                wsT = sbb("wsT", [128, 8, 128], BF16)
                wsT_buf = NB("wsT")
                vgain = sbb("vgain", [128, 1024], F32)
                bsb = sbb("bsb", [128, 8, 128], F32)
                cb_buf = NB("sgu_consts")
                sg_sem = new_dsem("sguc")
                sg2_sem = new_dsem("sguc2")
                GQ.dma([(wsT[:], wsT_d)], sg_sem, writes=[wsT_buf])
                SP.dma([(vgain[:], vgain_d), (bsb[:], bs_d.rearrange("p (g q) -> p g q", g=8))], sg2_sem,
                       writes=[cb_buf])
                zvt = [sbb(f"zvt{i}", [128, 1024], F32) for i in range(4)]
                zvt_bufs = [NB(f"zvt{i}") for i in range(4)]
                vn = [sbb(f"vn{i}", [128, 1024], BF16) for i in range(2)]
                vn_bufs = [NB(f"vn{i}") for i in range(2)]
                sqj = sbb("sqj", [128, 1024], BF16)
                sqj_holder[0] = NB("sqj")

                zs = [ws.get() for _ in range(4)]
                sg_ss = {}
                sg_rs = {}
                last_sq = [None]

                def sgu_s1(t):
                    zb = t % 4
                    bks = [nb(), nb()]
                    reads = flat([slot_bufs[z] for z in zs], hT_bufs[t])
                    PE.pre(reads=reads, writes=[b for _, b in bks])
                    for cbi in range(2):
                        for kc in range(16):
                            wv = v3(slots[zs[cbi * 2 + kc // 8]], 8, 512)
                            inst = nc.tensor.matmul(bks[cbi][0][:, :], lhsT=hT[:, kc, t * 128:(t + 1) * 128],
                                                    rhs=wv[:, kc % 8, :], start=(kc == 0), stop=(kc == 15))
                    PE.post(inst, reads=reads, writes=[b for _, b in bks])
                    for cbi in range(2):
                        ACT.op(lambda cbi=cbi: nc.scalar.activation(out=zvt[zb][:, cbi * 512:(cbi + 1) * 512],
                                                                    in_=bks[cbi][0][:, :], func=gelu_f),
                               reads=[bks[cbi][1]], writes=[zvt_bufs[zb]])
                    sg_ss[t] = sumsq(zvt[zb][:], [zvt_bufs[zb]], width=1024, junk_ap=sqj[:])

                def sgu_s2(t):
                    sg_rs[t] = rstd_from_ss(*sg_ss[t], 1024)

                def sgu_s3a(t):
                    zb = t % 4
                    s = t % 2
                    rs_ap, rs_buf = sg_rs[t]
                    scale_to(vn[s][:], [vn_bufs[s]], zvt[zb][:], [zvt_bufs[zb]], rs_ap, rs_buf, vgain[:], cb_buf)

                def sgu_s3(t):
                    s = t % 2
                    for b2 in range(2):
                        bank, bbuf = nb()
                        PE.pre(reads=[vn_bufs[s], wsT_buf], writes=[bbuf])
                        for i in range(4):
                            g = b2 * 4 + i
                            inst = nc.tensor.matmul(bank[:, i * 128:(i + 1) * 128],
                                                    lhsT=vn[s][:, g * 128:(g + 1) * 128], rhs=wsT[:, g, :],
                                                    start=True, stop=True)
                        PE.post(inst, reads=[vn_bufs[s], wsT_buf], writes=[bbuf])
                        ub = [u_bufs[b2 * 4 + i][t // 4] for i in range(4)]
                        DVE.op(lambda: nc.vector.tensor_tensor(out=b4(bank), in0=b4(bank),
                                                               in1=bsb[:, b2 * 4:(b2 + 1) * 4, :], op=ALU.add),
                               reads=[bbuf, cb_buf], writes=[bbuf])
                        usl = uA[:, b2 * 4:(b2 + 1) * 4, t * 128:(t + 1) * 128]
                        DVE.op(lambda: nc.vector.tensor_tensor(out=usl, in0=b4(bank), in1=usl, op=ALU.mult),
                               reads=flat(bbuf, ub), writes=ub)

                sgu_s1(0)
                sgu_s1(1)
                for p in range(4):
                    sgu_s2(2 * p)
                    sgu_s2(2 * p + 1)
                    if p + 1 < 4:
                        sgu_s1(2 * p + 2)
                        sgu_s1(2 * p + 3)
                    else:
                        s_ga0 = ws.get()
                        w_ = v3(slots[s_ga0], 16, 256)
                        bk_ = [nb(), nb()]
                        rd_ = flat(slot_bufs[s_ga0], hT_half)
                        PE.pre(reads=rd_, writes=[b for _, b in bk_])
                        for kc in range(16):
                            for hf in range(2):
                                inst = nc.tensor.matmul(bk_[hf][0][:, :], lhsT=w_[:, kc, 0:128],
                                                        rhs=hT[:, kc, hf * 512:(hf + 1) * 512],
                                                        start=(kc == 0), stop=(kc == 15))
                        PE.post(inst, reads=rd_, writes=[b for _, b in bk_])
                        early_ga["slot"] = s_ga0
                        early_ga["bk"] = bk_
                    sgu_s3a(2 * p)
                    sgu_s3a(2 * p + 1)
                    sgu_s3(2 * p)
                    sgu_s3(2 * p + 1)
                for z in zs:
                    ws.release(z)
                dbg_dump("A", uA[:], [128, 8, T], u_bufs)
                retire(wsT_buf, cb_buf, zvt_bufs, vn_bufs)


        b2 = Bump(Z0)
        mT = b2.take([128, 16, T], BF16)
        MT_END = b2.p
        mT_bufs = [[NB(f"mT{t}_{q}") for q in range(4)] for t in range(8)]

        def mT_bufs_for(j, hf):
            return [mT_bufs[t][j // 4] for t in range(hf * 4, hf * 4 + 4)]

        with contextlib.ExitStack() as s2:
            sb2 = mk_alloc(b2)

            sig = [sb2(f"sig{i}", [128, 512], F32) for i in range(4)]
            sig_bufs = [NB(f"sig{i}") for i in range(4)]
            t1 = [sb2(f"t1_{i}", [128, 512], F32) for i in range(4)]
            t1_bufs = [NB(f"t1_{i}") for i in range(4)]
            A_all = flat(u_bufs)
            B_all = flat(B_bufs)
            B_half = [flat([B_bufs[kh][n] for kh in range(2) for n in range(hf * 4, hf * 4 + 4)]) for hf in range(2)]
            A_half = [[u_bufs[g][hf] for g in range(8)] for hf in range(2)]
            sctr = 0
            for jj in range(8):
                s_ga = early_ga["slot"] if jj == 0 else ws.get()
                s_gb = ws.get()
                s_ab = ws.get()
                for c in range(2):
                    j = jj * 2 + c
                    res = {}
                    for name, si, nk, off, src, src_bufs in (
                        ("ga", s_ga, 16, 0, hT, hT_half),
                        ("gb", s_gb, 16, 0, hT, hT_half),
                        ("ya", s_ab, 8, 0, uA, A_half),
                        ("yb", s_ab, 8, 2048, Bt, B_half),
                    ):
                        w = v3(slots[si], nk, 256, off)
                        if j == 0 and name == "ga":
                            bk = early_ga["bk"]
                        else:
                            bk = [nb(), nb()]
                            reads = flat(slot_bufs[si], src_bufs)
                            PE.pre(reads=reads, writes=[b for _, b in bk])
                            for kc in range(nk):
                                for hf in range(2):
                                    inst = nc.tensor.matmul(bk[hf][0][:, :], lhsT=w[:, kc, c * 128:(c + 1) * 128],
                                                            rhs=src[:, kc, hf * 512:(hf + 1) * 512],
                                                            start=(kc == 0), stop=(kc == nk - 1))
                            PE.post(inst, reads=reads, writes=[b for _, b in bk])
                        res[name] = bk
                        if name in ("ga", "gb"):
                            for hf in range(2):
                                k_ = (0 if name == "ga" else 2) + hf
                                ACT.op(lambda: nc.scalar.activation(out=sig[k_][:], in_=bk[hf][0][:, :], func=AF.Sigmoid),
                                       reads=[bk[hf][1]], writes=[sig_bufs[k_]])
                        elif name == "ya":
                            for hf in range(2):
                                DVE.op(lambda: nc.vector.tensor_tensor(out=t1[hf][:], in0=sig[hf][:], in1=bk[hf][0][:, :],
                                                                       op=ALU.mult),
                                       reads=[sig_bufs[hf], bk[hf][1]], writes=[t1_bufs[hf]])
                        else:
                            for hf in range(2):
                                DVE.op(lambda: nc.vector.tensor_tensor(out=t1[2 + hf][:], in0=sig[2 + hf][:],
                                                                       in1=bk[hf][0][:, :], op=ALU.mult),
                                       reads=[sig_bufs[2 + hf], bk[hf][1]], writes=[t1_bufs[2 + hf]])
                                DVE.op(lambda: nc.vector.tensor_tensor(out=mT[:, j, hf * 512:(hf + 1) * 512],
                                                                       in0=t1[hf][:], in1=t1[2 + hf][:], op=ALU.add),
                                       reads=[t1_bufs[hf], t1_bufs[2 + hf]], writes=mT_bufs_for(j, hf))
                ws.release(s_ga)
                ws.release(s_gb)
                ws.release(s_ab)
            dbg_dump("mT", mT[:], [128, 16, T], mT_bufs)
            retire(hT_bufs, u_bufs, B_bufs, sig_bufs, t1_bufs)

    bd = Bump(R0)
    x1 = bd.take([128, NT, D], F32)
    sgt01 = [bd.take([128, 512], F32) for i in range(2)]
    assert bd.p <= Z0, (bd.p, Z0)
    x1_bufs = [[NB(f"x1_{t}_{n}") for n in range(4)] for t in range(NT)]
    x1_sem = new_dsem("x1ld")
    for n in range(4):
        for t in range(NT):
            SP.dma([(x1[:, t, n * 512:(n + 1) * 512], xc[t * 128:(t + 1) * 128, n * 512:(n + 1) * 512])],
                   new_dsem(f"x1ld{n}_{t}"), writes=[x1_bufs[t][n]])

    bd2 = Bump(MT_END)
    hb2 = [bd2.take([128, D], BF16) for i in range(2)]
    hb2_bufs = [NB(f"hb2_{i}") for i in range(2)]

    d2_ss = {}
    d2_rs = {}

    d2_tr = {}

    def d2_stage_a(t):
        d2_rs[t] = rs_ts(*d2_ss[t], D)
        rs_sqrt(*d2_rs[t])

    def d2_stage_b(t):
        rs_recip(*d2_rs[t])
        scale_to(hb2[t % 2][:], [hb2_bufs[t % 2]], x1[:, t, :], x1_bufs[t], d2_rs[t][0], d2_rs[t][1], gb[:], gb_buf)

    def d2_stage_c(t):
        d2_tr[t] = tr_pe(hb2[t % 2], hb2_bufs[t % 2])
        for q4 in range(4):
            tr_copy(d2_tr[t], q4, mT, mT_bufs[t], t, q4 == 3)

    for n in range(4):
        sl = [ws.get(), ws.get()]
        for t in range(NT):
            bank, bbuf = nb()
            reads = flat([slot_bufs[s_] for s_ in sl], mT_bufs[t])
            PE.pre(reads=reads, writes=[bbuf])
            for kc in range(16):
                wv = v3(slots[sl[kc // 8]], 8, 512)
                inst = nc.tensor.matmul(bank[:, :], lhsT=mT[:, kc, t * 128:(t + 1) * 128], rhs=wv[:, kc % 8, :],
                                        start=(kc == 0), stop=(kc == 15))
            PE.post(inst, reads=reads, writes=[bbuf])
            xs = x1[:, t, n * 512:(n + 1) * 512]
            DVE.op(lambda: nc.vector.tensor_tensor(out=xs, in0=xs, in1=bank[:, :], op=ALU.add),
                   reads=[bbuf, x1_bufs[t][n]], writes=[x1_bufs[t][n]])
            if n == 3:
                if t == 0:
                    SP.dma([(gb[:], g_ffn_d)], gb_sem, writes=[gb_buf])
                if t >= 1:
                    d2_stage_a(t - 1)
                d2_ss[t] = sumsq(x1[:, t, :], x1_bufs[t])
                if t >= 1:
                    d2_stage_b(t - 1)
                if t >= 2:
                    d2_stage_c(t - 2)
        for s_ in sl:
            ws.release(s_)
    d2_stage_a(NT - 1)
    d2_stage_c(NT - 2)
    early_ffn = {}
    s_g0 = ws.get()
    s_u0 = ws.get()
    for name_, si_ in (("g", s_g0), ("u", s_u0)):
        w_ = v3(slots[si_], 16, 256)
        bank_, bbuf_ = nb()
        rd_ = flat(slot_bufs[si_], mT_bufs[0:4])
        PE.pre(reads=rd_, writes=[bbuf_])
        for kc in range(16):
            inst = nc.tensor.matmul(bank_[:, :], lhsT=w_[:, kc, 0:128], rhs=mT[:, kc, 0:512],
                                    start=(kc == 0), stop=(kc == 15))
        PE.post(inst, reads=rd_, writes=[bbuf_])
        early_ffn[name_] = (bank_, bbuf_)
    d2_stage_b(NT - 1)
    d2_stage_c(NT - 1)
    dbg_dump("x1", x1[:], [128, NT, D], x1_bufs, F32)
    dbg_dump("h2T", mT[:], [128, 16, T], mT_bufs)
    h2T = mT
    h2_half = [flat(mT_bufs[0:4]), flat(mT_bufs[4:8])]
    retire(hb2_bufs)

    bf_ = Bump(MT_END)
    actT = bf_.take([128, 12, T], BF16)
    act_bufs = [[NB(f"act{f}_{hf}") for hf in range(2)] for f in range(12)]
    sgt = sgt01 + [bf_.take([128, 512], F32) for i in range(2)]
    sgt_bufs = [NB(f"sgt{i}") for i in range(4)]
    sgc = 0
    fin_ss = {}
    out_sem = new_dsem("out")
    out_toks = []

    def fin_stage_b(t):
        rs_ap, rs_buf = rstd_from_ss(*fin_ss[t], D)
        scale_to(x1[:, t, :], x1_bufs[t], x1[:, t, :], x1_bufs[t], rs_ap, rs_buf, gb[:], gb_buf)
        out_toks.append(SP.dma([(y[t * 128:(t + 1) * 128, :], x1[:, t, :])], out_sem, reads=x1_bufs[t]))

    for gi, gsz in enumerate(FFN_GROUPS):
        last_group = (gi == len(FFN_GROUPS) - 1)
        for p in range(gsz // 2):
            if gi == 0 and p == 0:
                s_g, s_u = s_g0, s_u0
            else:
                s_g = ws.get()
                s_u = ws.get()
            if gi == 0 and p == 0:
                for hf in range(2):
                    for c in range(2):
                        fi = c
                        k_ = None
                        for name, si in (("g", s_g), ("u", s_u)):
                            w = v3(slots[si], 16, 256)
                            if hf == 0 and c == 0:
                                bank, bbuf = early_ffn[name]
                            else:
                                bank, bbuf = nb()
                                reads = flat(slot_bufs[si], h2_half[hf])
                                PE.pre(reads=reads, writes=[bbuf])
                                for kc in range(16):
                                    inst = nc.tensor.matmul(bank[:, :], lhsT=w[:, kc, c * 128:(c + 1) * 128],
                                                            rhs=h2T[:, kc, hf * 512:(hf + 1) * 512],
                                                            start=(kc == 0), stop=(kc == 15))
                                PE.post(inst, reads=reads, writes=[bbuf])
                            if name == "g":
                                k_ = sgc % 4
                                sgc += 1
                                ACT.op(lambda: nc.scalar.activation(out=sgt[k_][:], in_=bank[:, :], func=AF.Silu),
                                       reads=[bbuf], writes=[sgt_bufs[k_]])
                            else:
                                DVE.op(lambda: nc.vector.tensor_tensor(out=actT[:, fi, hf * 512:(hf + 1) * 512],
                                                                       in0=sgt[k_][:], in1=bank[:, :], op=ALU.mult),
                                       reads=[sgt_bufs[k_], bbuf], writes=[act_bufs[fi][hf]])
                ws.release(s_g)
                ws.release(s_u)
                continue
            for c in range(2):
                fi = 2 * p + c
                for name, si in (("g", s_g), ("u", s_u)):
                    w = v3(slots[si], 16, 256)
                    bk = [nb(), nb()]
                    reads = flat(slot_bufs[si], h2_half)
                    PE.pre(reads=reads, writes=[b for _, b in bk])
                    for kc in range(16):
                        for hf in range(2):
                            inst = nc.tensor.matmul(bk[hf][0][:, :], lhsT=w[:, kc, c * 128:(c + 1) * 128],
                                                    rhs=h2T[:, kc, hf * 512:(hf + 1) * 512],
                                                    start=(kc == 0), stop=(kc == 15))
                    PE.post(inst, reads=reads, writes=[b for _, b in bk])
                    if name == "g":
                        ks = []
                        for hf in range(2):
                            k_ = sgc % 4
                            sgc += 1
                            ks.append(k_)
                            ACT.op(lambda: nc.scalar.activation(out=sgt[k_][:], in_=bk[hf][0][:, :], func=AF.Silu),
                                   reads=[bk[hf][1]], writes=[sgt_bufs[k_]])
                    else:
                        for hf in range(2):
                            k_ = ks[hf]
                            DVE.op(lambda: nc.vector.tensor_tensor(out=actT[:, fi, hf * 512:(hf + 1) * 512],
                                                                   in0=sgt[k_][:], in1=bk[hf][0][:, :], op=ALU.mult),
                                   reads=[sgt_bufs[k_], bk[hf][1]], writes=[act_bufs[fi][hf]])
            ws.release(s_g)
            ws.release(s_u)

        def down_group(n, t, sl):
            bank, bbuf = nb()
            reads = flat([slot_bufs[s_] for s_ in sl], [act_bufs[f][t // 4] for f in range(gsz)])
            PE.pre(reads=reads, writes=[bbuf])
            for f in range(gsz):
                if f < 8:
                    wv = v3(slots[sl[0]], 8, 512)[:, f, :]
                else:
                    wv = v3(slots[sl[1]], gsz - 8, 512)[:, f - 8, :]
                inst = nc.tensor.matmul(bank[:, :], lhsT=actT[:, f, t * 128:(t + 1) * 128], rhs=wv,
                                        start=(f == 0), stop=(f == gsz - 1))
            PE.post(inst, reads=reads, writes=[bbuf])
            xs = x1[:, t, n * 512:(n + 1) * 512]
            DVE.op(lambda: nc.vector.tensor_tensor(out=xs, in0=xs, in1=bank[:, :], op=ALU.add),
                   reads=[bbuf, x1_bufs[t][n]], writes=[x1_bufs[t][n]])

        if not last_group:
            for n in range(4):
                sl = [ws.get()]
                if gsz > 8:
                    sl.append(ws.get())
                for t in range(NT):
                    down_group(n, t, sl)
                for s_ in sl:
                    ws.release(s_)
        else:
            assert gsz <= 8
            SP.dma([(gb[:], g_fin_d)], gb_sem, writes=[gb_buf])
            sls = [[ws.get()] for n in range(4)]
            for t in range(NT):
                for n in range(4):
                    down_group(n, t, sls[n])
                if t >= 1:
                    fin_stage_b(t - 1)
                fin_ss[t] = sumsq(x1[:, t, :], x1_bufs[t])
            for n in range(4):
                ws.release(sls[n][0])
            fin_stage_b(NT - 1)

    SP.wait(out_toks[-1])
    es.close()
    return nc


def _t5_bucket_np(rel):
    nb = 16
    ret = np.where(rel > 0, nb, 0)
    n = np.abs(rel)
    me = 8
    nf = np.maximum(n, 1).astype(np.float32)
    large = me + (np.log(nf / np.float32(me)) / np.float32(math.log(128 / me)) * np.float32(nb - me)).astype(np.int32)
    large = np.minimum(large, nb - 1)
    return ret + np.where(n < me, n, large)


def _bias_table(rel_bias):
    j = np.arange(128)[:, None]
    i = np.arange(128)[None, :]
    out = np.empty((128, 3, 8, 128), np.float32)
    for ty, off in enumerate((-128, 0, 128)):
        rel = j + off - i
        valid = np.abs(rel) <= 128
        bk = _t5_bucket_np(rel)
        g = rel_bias[bk]
        g = np.where(valid[:, :, None], g, np.float32(NEG)).astype(np.float32)
        out[:, ty] = g.transpose(0, 2, 1)
    return out.reshape(128, 3 * 8 * 128)


def _rep(v, n=128):
    return np.ascontiguousarray(np.broadcast_to(np.asarray(v, np.float32).reshape(1, -1), (n, v.size)))


_NC_CACHE = {}


def make_in_maps(x, w_in, norm_mix, sgu_v_gain, sgu_w_s, sgu_b_s, w_a_out, attn_sink, rel_bias,
                 w_b_out, w_o, norm_ffn, w_gate, w_up, w_down, norm_final):
    f = lambda a: np.ascontiguousarray(np.asarray(a, dtype=np.float32))
    x = f(x)
    shared = {
        "w_in": f(w_in)[0], "w_a": f(w_a_out)[0], "w_b": f(w_b_out)[0], "w_o": f(w_o)[0],
        "w_gate": f(w_gate)[0], "w_up": f(w_up)[0], "w_down": f(w_down)[0],
        "wsT": np.ascontiguousarray(f(sgu_w_s)[0].transpose(2, 0, 1)),
        "g_mix_b": _rep(f(norm_mix)[0]), "g_ffn_b": _rep(f(norm_ffn)[0]), "g_fin_b": _rep(f(norm_final)),
        "vgain_b": _rep(f(sgu_v_gain)[0]), "bs_b": _rep(f(sgu_b_s)[0].reshape(-1)),
        "sink_b": _rep(f(attn_sink)[0]),
        "biasT": _bias_table(f(rel_bias)),
        "ident": np.eye(128, dtype=np.float32),
    }
    in_maps = []
    for c in range(NCORES):
        b, half = c // 2, c % 2
        own = x[b, half * 1024:(half + 1) * 1024]
        halo = x[b, 1024:1152] if half == 0 else x[b, 896:1024]
        edge = np.zeros((128, 2), np.float32)
        if half == 0:
            edge[:, 0] = NEG
        else:
            edge[:, 1] = NEG
        m = dict(shared)
        m["xc"] = np.ascontiguousarray(np.concatenate([own, halo], axis=0))
        m["edge"] = edge
        in_maps.append(m)
    return in_maps


def kernel(**inputs):
    in_maps = make_in_maps(**inputs)
    if "nc" not in _NC_CACHE:
        _NC_CACHE["nc"] = build_nc()
    nc = _NC_CACHE["nc"]
    res = run_bass_kernel_spmd(nc, in_maps, core_ids=list(range(NCORES)))
    out = np.empty((4, 2048, D), np.float32)
    for c in range(NCORES):
        b, half = c // 2, c % 2
        out[b, half * 1024:(half + 1) * 1024] = res.results[c]["y"]
    return out
```
